# Optimizing a Trainium2 kernel written in Bass

```python
import math
import jax, jax.numpy as jnp
from jax import lax
import numpy as np

D_MODEL = 1024
BATCH = 16
SEQ = 256
DEPTH = 4
DEC_BATCH = 8
DEC_SEQ = 4096
PAST_LEN = 512

GRID_W = 64
HEAD_DIM = 64
ATT_Q_HEADS = 8
ATT_KV_HEADS = 2
Q_BLOCK = 128
ROPE_BASE = 10000.0
ROPE_PAIRS = HEAD_DIM // 4
DN_HEADS = 4
DN_DK = 64
DN_DV = 64
DN_CHUNK = 64
LRU_WIDTH = 256
LRU_BLOCKS = 4
LRU_BLOCK_W = LRU_WIDTH // LRU_BLOCKS
LRU_C = 8.0
CONV_W = 4
CONV_PAD = (2, 1)
D_FF = 2816
N_BRANCH = 3
N_MOD = 9
EPS = 1e-6
IN_SIZES = (DN_HEADS * DN_DK, DN_HEADS * DN_DK, DN_HEADS * DN_DV, DN_HEADS * DN_DV, 2 * DN_HEADS, 2 * DN_HEADS,
            ATT_Q_HEADS * HEAD_DIM, ATT_KV_HEADS * HEAD_DIM, ATT_KV_HEADS * HEAD_DIM, LRU_WIDTH, LRU_WIDTH,
            N_BRANCH * D_MODEL)
N_IN = sum(IN_SIZES)

kernel_name = 'hybrid_diffusion_prefix_trunk_step'


def rmsnorm(x, w):
    xf = x.astype(jnp.float32)
    y = xf * lax.rsqrt(jnp.mean(xf * xf, axis=-1, keepdims=True) + EPS)
    return (y * w.astype(jnp.float32)).astype(x.dtype)


def l2norm(x):
    return x * lax.rsqrt(jnp.sum(x * x, axis=-1, keepdims=True) + EPS)


def swiglu(h, w_gu, w_down):
    g, u = jnp.split(h @ w_gu, 2, axis=-1)
    return (jax.nn.silu(g) * u) @ w_down


def depthwise_conv(x, w):
    C = x.shape[-1]
    return lax.conv_general_dilated(x, w[:, None, :].astype(x.dtype), window_strides=(1,), padding=[CONV_PAD],
                                    dimension_numbers=('NWC', 'WIO', 'NWC'), feature_group_count=C)


def axial_rope(n_tokens):
    rows = n_tokens // GRID_W
    row = jnp.broadcast_to(jnp.arange(rows, dtype=jnp.float32)[:, None], (rows, GRID_W)).reshape(-1)
    col = jnp.broadcast_to(jnp.arange(GRID_W, dtype=jnp.float32)[None, :], (rows, GRID_W)).reshape(-1)
    freqs = ROPE_BASE ** (-jnp.arange(ROPE_PAIRS, dtype=jnp.float32) / ROPE_PAIRS)
    ang = jnp.stack([row[:, None] * freqs, col[:, None] * freqs], axis=1)
    return jnp.cos(ang), jnp.sin(ang)


def apply_axial_rope(x, cos, sin):
    B, T, H, hd = x.shape
    xs = x.astype(jnp.float32).reshape(B, T, H, 2, 2, ROPE_PAIRS)
    x1, x2 = xs[..., 0, :], xs[..., 1, :]
    c = cos[None, :, None]
    s = sin[None, :, None]
    out = jnp.stack([x1 * c - x2 * s, x2 * c + x1 * s], axis=-2)
    return out.reshape(B, T, H, hd).astype(x.dtype)


def blocked_attention(q, k, v):
    B, T, Hq, hd = q.shape
    Hkv = k.shape[2]
    G = Hq // Hkv
    nb = T // Q_BLOCK
    qb = jnp.moveaxis(q.reshape(B, nb, Q_BLOCK, Hkv, G, hd), 1, 0)
    scale = hd ** -0.5

    def one_block(qblk):
        s = jnp.einsum('bqkgd,bskd->bkgqs', qblk, k, preferred_element_type=jnp.float32) * scale
        p = jax.nn.softmax(s, axis=-1).astype(v.dtype)
        return jnp.einsum('bkgqs,bskd->bqkgd', p, v)

    o = lax.map(one_block, qb)
    return jnp.moveaxis(o, 0, 1).reshape(B, T, Hq * hd)


def gated_delta_chunked(q, k, v, beta, g, s0):
    B, T, H, dk = q.shape
    dv = v.shape[-1]
    C = DN_CHUNK
    n = T // C
    to_c = lambda a: jnp.moveaxis(a.reshape(B, n, C, H, -1), 3, 2)
    qc, kc, vc = to_c(q), to_c(k), to_c(v)
    bc = to_c(beta[..., None])[..., 0]
    gc = jnp.cumsum(to_c(g[..., None])[..., 0], axis=-1)
    idx = jnp.arange(C)
    incl = idx[:, None] >= idx[None, :]
    strict = idx[:, None] > idx[None, :]
    diff = gc[..., :, None] - gc[..., None, :]
    dec_incl = jnp.exp(jnp.where(incl, diff, -jnp.inf))
    dec_strict = jnp.where(strict, dec_incl, 0.0)
    kk = jnp.einsum('bnhcd,bnhed->bnhce', kc, kc)
    a_mat = jnp.eye(C, dtype=jnp.float32) + bc[..., :, None] * kk * dec_strict
    eg = jnp.exp(gc)
    rhs = jnp.concatenate([bc[..., None] * vc, (bc * eg)[..., None] * kc], axis=-1)
    sol = lax.linalg.triangular_solve(a_mat, rhs, left_side=True, lower=True, unit_diagonal=True)
    u, wk = sol[..., :dv], sol[..., dv:]
    qk = jnp.einsum('bnhcd,bnhed->bnhce', qc, kc) * dec_incl
    q_dec = qc * eg[..., None]
    k_dec = kc * jnp.exp(gc[..., -1:] - gc)[..., None]
    g_tot = eg[..., -1]
    xs = tuple(jnp.moveaxis(a, 1, 0) for a in (u, wk, qk, q_dec, k_dec, g_tot))

    def step(S, inp):
        u_c, wk_c, qk_c, qd_c, kd_c, gt_c = inp
        w = u_c - jnp.einsum('bhck,bhkv->bhcv', wk_c, S)
        o = jnp.einsum('bhck,bhkv->bhcv', qd_c, S) + jnp.einsum('bhcs,bhsv->bhcv', qk_c, w)
        S = gt_c[..., None, None] * S + jnp.einsum('bhck,bhcv->bhkv', kd_c, w)
        return S, o

    s_fin, o = lax.scan(step, s0.astype(jnp.float32), xs)
    o = jnp.moveaxis(jnp.moveaxis(o, 0, 1), 2, 3).reshape(B, T, H, dv)
    return o, s_fin


def deltanet_branch(q_in, k_in, v_in, z, b_logit, a_logit, conv_w, a_log, dt_bias, onorm_w, s0):
    B, T, _ = q_in.shape
    qkv = jax.nn.silu(depthwise_conv(jnp.concatenate([q_in, k_in, v_in], axis=-1), conv_w)).astype(jnp.float32)
    q, k, v = jnp.split(qkv, [DN_HEADS * DN_DK, 2 * DN_HEADS * DN_DK], axis=-1)
    q = l2norm(q.reshape(B, T, DN_HEADS, DN_DK)) * (DN_DK ** -0.5)
    k = l2norm(k.reshape(B, T, DN_HEADS, DN_DK))
    v = v.reshape(B, T, DN_HEADS, DN_DV)
    beta = jax.nn.sigmoid(b_logit.astype(jnp.float32)).reshape(B, T, 2, DN_HEADS)
    g = -jnp.exp(a_log.astype(jnp.float32)) * jax.nn.softplus(
        a_logit.astype(jnp.float32).reshape(B, T, 2, DN_HEADS) + dt_bias.astype(jnp.float32))
    rev = lambda a: jnp.flip(a, axis=1)
    o_f, s_f = gated_delta_chunked(q, k, v, beta[:, :, 0], g[:, :, 0], s0[:, 0])
    o_b, s_b = gated_delta_chunked(rev(q), rev(k), rev(v), rev(beta[:, :, 1]), rev(g[:, :, 1]), s0[:, 1])
    o = o_f + rev(o_b)
    o = rmsnorm(o, onorm_w) * jax.nn.silu(z.astype(jnp.float32).reshape(B, T, DN_HEADS, DN_DV))
    return o.reshape(B, T, DN_HEADS * DN_DV).astype(q_in.dtype), jnp.stack([s_f, s_b], axis=1)


def linear_scan(a, u, h0):
    def comb(l, r):
        return (l[0] * r[0], r[0] * l[1] + r[1])
    a_cum, b_cum = lax.associative_scan(comb, (a, u), axis=1)
    h = a_cum * h0[:, None, :] + b_cum
    return h, h[:, -1]


def rglru_branch(x_in, y_in, conv_w, conv_b, w_r, b_r, w_i, b_i, lam, h0):
    B, T, W = x_in.shape
    x = (depthwise_conv(x_in, conv_w) + conv_b.astype(x_in.dtype)).astype(jnp.float32)
    xb = x.reshape(B, T, LRU_BLOCKS, LRU_BLOCK_W)
    h0 = h0.astype(jnp.float32)
    outs, finals = [], []
    for d in range(2):
        r = jax.nn.sigmoid(jnp.einsum('btnc,ncd->btnd', xb, w_r[d].astype(jnp.float32)).reshape(B, T, W)
                           + b_r[d].astype(jnp.float32))
        i = jax.nn.sigmoid(jnp.einsum('btnc,ncd->btnd', xb, w_i[d].astype(jnp.float32)).reshape(B, T, W)
                           + b_i[d].astype(jnp.float32))
        log_a = -LRU_C * r * jax.nn.softplus(-lam[d].astype(jnp.float32))
        a = jnp.exp(log_a)
        u = jnp.sqrt(-jnp.expm1(2.0 * log_a)) * (i * x)
        if d == 1:
            a, u = jnp.flip(a, axis=1), jnp.flip(u, axis=1)
        h, h_last = linear_scan(a, u, h0[:, d])
        if d == 1:
            h = jnp.flip(h, axis=1)
        outs.append(h)
        finals.append(h_last)
    out = jax.nn.gelu(y_in.astype(jnp.float32)) * (outs[0] + outs[1])
    return out.astype(x_in.dtype), jnp.stack(finals, axis=1)


def token_mix(h, lp, rope, ctx):
    B, T, _ = h.shape
    split_at = [int(s) for s in np.cumsum(IN_SIZES)[:-1]]
    dq, dk, dv, dz, db, da, aq, ak, av, lx, ly, gl = jnp.split(h @ lp['w_in'], split_at, axis=-1)
    if ctx is None:
        s_dn0 = jnp.zeros((B, 2, DN_HEADS, DN_DK, DN_DV), jnp.float32)
        h_lru0 = jnp.zeros((B, 2, LRU_WIDTH), jnp.float32)
    else:
        k_ctx, v_ctx, s_dn0, h_lru0 = ctx
    o_dn, s_dn = deltanet_branch(dq, dk, dv, dz, db, da, lp['dn_conv_w'], lp['dn_a_log'], lp['dn_dt_bias'],
                                 lp['dn_onorm_w'], s_dn0)
    q = rmsnorm(aq.reshape(B, T, ATT_Q_HEADS, HEAD_DIM), lp['att_qnorm_w'])
    k = rmsnorm(ak.reshape(B, T, ATT_KV_HEADS, HEAD_DIM), lp['att_knorm_w'])
    v = av.reshape(B, T, ATT_KV_HEADS, HEAD_DIM)
    if ctx is None:
        o_att = blocked_attention(q, k, v)
    else:
        cos, sin = rope
        o_att = blocked_attention(apply_axial_rope(q, cos, sin),
                                  jnp.concatenate([k_ctx.astype(k.dtype), apply_axial_rope(k, cos, sin)], axis=1),
                                  jnp.concatenate([v_ctx.astype(v.dtype), v], axis=1))
    o_lru, h_lru = rglru_branch(lx, ly, lp['lru_conv_w'], lp['lru_conv_b'], lp['lru_wr'], lp['lru_br'],
                                lp['lru_wi'], lp['lru_bi'], lp['lru_lam'], h_lru0)
    gates = jax.nn.sigmoid(gl.astype(jnp.float32)).astype(h.dtype).reshape(B, T, N_BRANCH, D_MODEL)
    merged = (gates[:, :, 0] * (o_dn @ lp['w_pa'])
              + gates[:, :, 1] * (o_att @ lp['w_pb'])
              + gates[:, :, 2] * (o_lru @ lp['w_pc']))
    out = merged @ lp['w_o']
    new_ctx = (k, v, s_dn, h_lru) if ctx is None else None
    return out, new_ctx


def run_layer(x, cond, lp, rope, ctx):
    mod = (jax.nn.silu(cond) @ lp['w_mod'] + lp['b_mod']).reshape(cond.shape[0], N_MOD, 1, D_MODEL)
    nw = lp['norm_w']
    h = rmsnorm(x, nw[0]) * (1 + mod[:, 1]) + mod[:, 0]
    x = x + 0.5 * mod[:, 2] * swiglu(h, lp['ffn1_wgu'], lp['ffn1_wd'])
    h = rmsnorm(x, nw[1]) * (1 + mod[:, 4]) + mod[:, 3]
    out, new_ctx = token_mix(h, lp, rope, ctx)
    x = x + mod[:, 5] * out
    h = rmsnorm(x, nw[2]) * (1 + mod[:, 7]) + mod[:, 6]
    x = x + 0.5 * mod[:, 8] * swiglu(h, lp['ffn2_wgu'], lp['ffn2_wd'])
    return x, new_ctx


def setup_inputs(seed: int = 0) -> dict:
    key = jax.random.key(seed)
    ks = list(jax.random.split(key, 40))
    f32 = jnp.float32
    nrm = lambda shape, s: jax.random.normal(ks.pop(), shape, f32) * s
    L = DEPTH
    dn_qkv = 2 * DN_HEADS * DN_DK + DN_HEADS * DN_DV
    a0 = jax.random.uniform(ks.pop(), (L, 2, LRU_WIDTH), f32, 0.9, 0.999)
    p = a0 ** (1.0 / LRU_C)
    lam = jnp.log(p) - jnp.log1p(-p)
    dt = jnp.exp(jax.random.uniform(ks.pop(), (L, 2, DN_HEADS), f32, math.log(1e-3), math.log(1e-1)))
    dt_bias = dt + jnp.log(-jnp.expm1(-dt))
    a_log = jnp.log(jax.random.uniform(ks.pop(), (L, 2, DN_HEADS), f32, 1.0, 16.0))
    return {
        'x_prompt': nrm((BATCH, SEQ, D_MODEL), 1.0),
        'x_sample': nrm((DEC_BATCH, DEC_SEQ, D_MODEL), 1.0),
        'cache_k': nrm((DEC_BATCH, L, PAST_LEN, ATT_KV_HEADS, HEAD_DIM), 1.0),
        'cache_v': nrm((DEC_BATCH, L, PAST_LEN, ATT_KV_HEADS, HEAD_DIM), 1.0),
        'state_delta': nrm((DEC_BATCH, L, 2, DN_HEADS, DN_DK, DN_DV), 0.1),
        'state_lru': nrm((DEC_BATCH, L, 2, LRU_WIDTH), 0.5),
        'c': nrm((DEC_BATCH, D_MODEL), 1.0),
        'c_ctx': nrm((D_MODEL,), 1.0),
        'w_mod': nrm((L, D_MODEL, N_MOD * D_MODEL), 0.5 * D_MODEL ** -0.5),
        'b_mod': nrm((L, N_MOD * D_MODEL), 0.01),
        'norm_w': 1.0 + nrm((L, 3, D_MODEL), 0.02),
        'ffn1_wgu': nrm((L, D_MODEL, 2 * D_FF), D_MODEL ** -0.5),
        'ffn1_wd': nrm((L, D_FF, D_MODEL), D_FF ** -0.5),
        'ffn2_wgu': nrm((L, D_MODEL, 2 * D_FF), D_MODEL ** -0.5),
        'ffn2_wd': nrm((L, D_FF, D_MODEL), D_FF ** -0.5),
        'w_in': nrm((L, D_MODEL, N_IN), D_MODEL ** -0.5),
        'dn_conv_w': nrm((L, CONV_W, dn_qkv), CONV_W ** -0.5),
        'dn_a_log': a_log,
        'dn_dt_bias': dt_bias,
        'dn_onorm_w': 1.0 + nrm((L, DN_DV), 0.02),
        'att_qnorm_w': 1.0 + nrm((L, HEAD_DIM), 0.02),
        'att_knorm_w': 1.0 + nrm((L, HEAD_DIM), 0.02),
        'lru_conv_w': nrm((L, CONV_W, LRU_WIDTH), CONV_W ** -0.5),
        'lru_conv_b': nrm((L, LRU_WIDTH), 0.01),
        'lru_wr': nrm((L, 2, LRU_BLOCKS, LRU_BLOCK_W, LRU_BLOCK_W), LRU_BLOCK_W ** -0.5),
        'lru_br': nrm((L, 2, LRU_WIDTH), 0.01),
        'lru_wi': nrm((L, 2, LRU_BLOCKS, LRU_BLOCK_W, LRU_BLOCK_W), LRU_BLOCK_W ** -0.5),
        'lru_bi': nrm((L, 2, LRU_WIDTH), 0.01),
        'lru_lam': lam,
        'w_pa': nrm((L, DN_HEADS * DN_DV, D_MODEL), (DN_HEADS * DN_DV) ** -0.5),
        'w_pb': nrm((L, ATT_Q_HEADS * HEAD_DIM, D_MODEL), (ATT_Q_HEADS * HEAD_DIM) ** -0.5),
        'w_pc': nrm((L, LRU_WIDTH, D_MODEL), LRU_WIDTH ** -0.5),
        'w_o': nrm((L, D_MODEL, D_MODEL), D_MODEL ** -0.5),
    }


def reference(x_prompt, x_sample, cache_k, cache_v, state_delta, state_lru, c, c_ctx, w_mod, b_mod, norm_w,
              ffn1_wgu, ffn1_wd, ffn2_wgu, ffn2_wd, w_in, dn_conv_w, dn_a_log, dn_dt_bias, dn_onorm_w,
              att_qnorm_w, att_knorm_w, lru_conv_w, lru_conv_b, lru_wr, lru_br, lru_wi, lru_bi, lru_lam,
              w_pa, w_pb, w_pc, w_o):
    rope = axial_rope(x_sample.shape[1])
    xp, xs = x_prompt, x_sample
    new_k, new_v, new_sd, new_sl = [], [], [], []
    for l in range(DEPTH):
        lp = {
            'w_mod': w_mod[l], 'b_mod': b_mod[l], 'norm_w': norm_w[l],
            'ffn1_wgu': ffn1_wgu[l], 'ffn1_wd': ffn1_wd[l], 'ffn2_wgu': ffn2_wgu[l], 'ffn2_wd': ffn2_wd[l],
            'w_in': w_in[l], 'dn_conv_w': dn_conv_w[l], 'dn_a_log': dn_a_log[l], 'dn_dt_bias': dn_dt_bias[l],
            'dn_onorm_w': dn_onorm_w[l], 'att_qnorm_w': att_qnorm_w[l], 'att_knorm_w': att_knorm_w[l],
            'lru_conv_w': lru_conv_w[l], 'lru_conv_b': lru_conv_b[l], 'lru_wr': lru_wr[l], 'lru_br': lru_br[l],
            'lru_wi': lru_wi[l], 'lru_bi': lru_bi[l], 'lru_lam': lru_lam[l],
            'w_pa': w_pa[l], 'w_pb': w_pb[l], 'w_pc': w_pc[l], 'w_o': w_o[l],
        }
        xp, (k_l, v_l, sd_l, sl_l) = run_layer(xp, c_ctx[None, :], lp, None, None)
        new_k.append(k_l)
        new_v.append(v_l)
        new_sd.append(sd_l)
        new_sl.append(sl_l)
        xs, _ = run_layer(xs, c, lp, rope, (cache_k[:, l], cache_v[:, l], state_delta[:, l], state_lru[:, l]))
    return (xp, xs, jnp.stack(new_k, axis=1), jnp.stack(new_v, axis=1), jnp.stack(new_sd, axis=1),
            jnp.stack(new_sl, axis=1))
```

```python
import numpy as np
import concourse.bass as bass
import concourse.mybir as mybir
from concourse.bass_utils import run_bass_kernel_spmd

F32 = mybir.dt.float32
BF16 = mybir.dt.bfloat16
AF = mybir.ActivationFunctionType
ALU = mybir.AluOpType
AX = mybir.AxisListType

D = 1024
DFF = 2816
EPS = 1e-6
NIN = 5392
ENGS = ('pe', 'act', 'dve', 'pool', 'sp')


class Buf:
    def __init__(self, name, h):
        self.name = name
        self.h = h
        self.lw = None
        self.rd = {}
        self.sem = None
        self.dcnt = 0
        self.gen = 0
        self.psum = False

    def __getitem__(self, idx):
        return self.h[idx]


class Prog:
    def __init__(self, nc):
        self.nc = nc
        self.ops = {e: [] for e in ENGS}
        self.cnt = {e: 0 for e in ENGS}
        self.waited = {e: {} for e in ENGS}
        self.sems = []
        self.esem = {}
        for e in ENGS:
            self.esem[e] = self.newsem('e_' + e)
        self.dmabufs = []
        self.semcache = {}

    def newsem(self, name):
        s = self.nc.alloc_semaphore(name)
        self.sems.append(s)
        return len(self.sems) - 1

    def _need(self, eng, reads, writes, extra=()):
        need = {}

        def add(tok):
            if tok is None:
                return
            si, val, te = tok
            if te == 'pe' and eng == 'pe':
                return
            if need.get(si, 0) < val:
                need[si] = val
        for b in reads:
            add(b.lw)
            if b.psum:
                for si, (val, te) in b.rd.items():
                    if te != eng:
                        add((si, val, te))
        for b in writes:
            add(b.lw)
            for si, (val, te) in b.rd.items():
                add((si, val, te))
        for t in extra:
            add(t)
        w = self.waited[eng]
        out = []
        for si, val in need.items():
            if w.get(si, 0) < val:
                w[si] = val
                out.append((si, val))
        return out

    def _upd(self, tok, reads, writes):
        si, val, te = tok
        for b in reads:
            b.rd[si] = (val, te)
        for b in writes:
            b.lw = tok
            b.rd = {}

    def op(self, eng, fn, reads=(), writes=()):
        waits = self._need(eng, reads, writes)
        self.cnt[eng] += 1
        tok = (self.esem[eng], self.cnt[eng], eng)
        self._upd(tok, reads, writes)
        self.ops[eng].append((waits, fn, self.esem[eng], 1))

    def dma(self, q, out, in_, reads, writes, anchor, serialize=True, **kw):
        if anchor.sem is None:
            if anchor.name not in self.semcache:
                self.semcache[anchor.name] = [self.newsem('d_' + anchor.name), 0]
            anchor.sem = self.semcache[anchor.name][0]
        ent = self.semcache[anchor.name]
        extra = []
        if ent[1] and serialize:
            extra.append((anchor.sem, ent[1], 'dma'))
        waits = self._need(q, reads, writes, extra)
        ent[1] += 16
        anchor.dcnt = ent[1]
        tok = (anchor.sem, anchor.dcnt, 'dma')
        self._upd(tok, reads, writes)
        self.ops[q].append((waits, (lambda e: e.dma_start(out=out, in_=in_, **kw)), anchor.sem, 16))

    def barrier(self):
        toks = [(self.esem[e], self.cnt[e]) for e in ENGS if self.cnt[e]]
        toks += [(v[0], v[1]) for v in self.semcache.values()]
        for e in ENGS:
            w = self.waited[e]
            lst = []
            for si, val in toks:
                if si == self.esem[e]:
                    continue
                if w.get(si, 0) < val:
                    w[si] = val
                    lst.append((si, val))
            if lst:
                self.ops[e].append((lst, None, None, 0))

    def emit(self):
        nc = self.nc
        sems = self.sems
        ops = self.ops

        def run(e, lst):
            for waits, fn, si, inc in lst:
                for wsi, val in waits:
                    e.wait_ge(sems[wsi], val)
                if fn is not None:
                    fn(e).then_inc(sems[si], inc)
        with nc.Block() as block:
            @block.tensor
            def _(e):
                run(e, ops['pe'])

            @block.scalar
            def _(e):
                run(e, ops['act'])

            @block.vector
            def _(e):
                run(e, ops['dve'])

            @block.gpsimd
            def _(e):
                run(e, ops['pool'])

            @block.sync
            def _(e):
                run(e, ops['sp'])


class FreeList:
    def __init__(self, bufs):
        self.free = list(bufs)

    def get(self):
        return self.free.pop(0)

    def put(self, b):
        self.free.append(b)


def wstream_layout():
    lay = {}
    off = 0

    def add(name, n):
        nonlocal off
        lay[name] = (off, n)
        off += n
    for f in (1, 2):
        for b in range(11):
            add('gu%d_%d' % (f, b), 8 * 512)
        for mo in range(8):
            add('wd%d_%d' % (f, mo), 22 * 128)
    for i, n in enumerate((512, 512, 256)):
        add('infm_%d' % i, 8 * n)
    for i, n in enumerate((272, 512, 256)):
        add('intm_%d' % i, 8 * n)
    for b in range(3):
        for hf in range(2):
            add('gate_%d_%d' % (b, hf), 8 * 512)
    add('pa', 2 * 1024)
    add('pb', 4 * 1024)
    add('pc', 2 * 1024)
    for hf in range(2):
        add('wo_%d' % hf, 8 * 512)
    for b in range(18):
        add('mod_%d' % b, 8 * 512)
    return lay, off


def kc_layout(w, kc):
    return np.ascontiguousarray(w.reshape(kc, 128, w.shape[1]).transpose(1, 0, 2)).reshape(128, -1)


def build_wstream(inp, l):
    lay, tot = wstream_layout()
    ws = np.zeros((128, tot), np.float32)

    def put(name, arr):
        o, n = lay[name]
        assert arr.shape == (128, n), (name, arr.shape, n)
        ws[:, o:o + n] = arr
    ffw = {1: (inp['ffn1_wgu'], inp['ffn1_wd']), 2: (inp['ffn2_wgu'], inp['ffn2_wd'])}
    for f in (1, 2):
        wgu = ffw[f][0][l]
        g, u = wgu[:, :DFF], wgu[:, DFF:]
        for b in range(11):
            cols = np.concatenate([g[:, 256 * b:256 * b + 128], u[:, 256 * b:256 * b + 128],
                                   g[:, 256 * b + 128:256 * b + 256], u[:, 256 * b + 128:256 * b + 256]], axis=1)
            put('gu%d_%d' % (f, b), kc_layout(cols, 8))
        wd = ffw[f][1][l]
        for mo in range(8):
            put('wd%d_%d' % (f, mo), kc_layout(wd[:, mo * 128:(mo + 1) * 128], 22))
    w = inp['w_in'][l]
    dq, dk, dv, dz, db, da = w[:, 0:256], w[:, 256:512], w[:, 512:768], w[:, 768:1024], w[:, 1024:1032], w[:, 1032:1040]
    aq, ak, av, lx, ly = w[:, 1040:1552], w[:, 1552:1680], w[:, 1680:1808], w[:, 1808:2064], w[:, 2064:2320]
    gl = w[:, 2320:]
    aq = aq.reshape(D, 8, 64)[:, [0, 4, 1, 5, 2, 6, 3, 7], :].reshape(D, 512)
    fm = np.concatenate([dq, dk, dv, lx, ly], axis=1)
    for i, (a, n) in enumerate(((0, 512), (512, 512), (1024, 256))):
        put('infm_%d' % i, kc_layout(fm[:, a:a + n], 8))
    tm = np.concatenate([dz, db, da, aq, ak, av], axis=1)
    for i, (a, n) in enumerate(((0, 272), (272, 512), (784, 256))):
        put('intm_%d' % i, kc_layout(tm[:, a:a + n], 8))
    for b in range(3):
        for hf in range(2):
            put('gate_%d_%d' % (b, hf), kc_layout(gl[:, b * 1024 + hf * 512:b * 1024 + hf * 512 + 512], 8))
    put('pa', kc_layout(inp['w_pa'][l], 2))
    wpb = inp['w_pb'][l]
    rows = np.zeros((4, 128), np.int64)
    for g_ in range(4):
        rows[g_, :64] = g_ * 64 + np.arange(64)
        rows[g_, 64:] = (4 + g_) * 64 + np.arange(64)
    put('pb', kc_layout(wpb[rows.reshape(-1)], 4))
    put('pc', kc_layout(inp['w_pc'][l], 2))
    wo = inp['w_o'][l]
    for hf in range(2):
        put('wo_%d' % hf, kc_layout(wo[:, hf * 512:(hf + 1) * 512], 8))
    wm = inp['w_mod'][l]
    for b in range(18):
        put('mod_%d' % b, kc_layout(wm[:, b * 512:(b + 1) * 512], 8))
    return ws


def make_consts():
    i = np.arange(128)
    I, J = i[:, None], i[None, :]
    c = {}
    c['ident'] = (I == J)
    c['ones'] = np.ones((128, 128))
    c['UI'] = (I <= J)
    c['LI'] = (I >= J)
    c['blk64'] = ((I // 64) == (J // 64))
    bd = lambda n: ((I // n) == (J // n))
    c['bd16'] = bd(16)
    c['off32'] = bd(32) & ~bd(16)
    c['off64'] = bd(64) & ~bd(32)
    c['off128'] = ~bd(64)
    SL = (I > J)
    SU = (I < J)
    c['ms8'] = np.concatenate([np.tile(SL, (1, 4)), np.tile(SU, (1, 4))], axis=1)
    c['mst8'] = np.concatenate([np.tile(SU, (1, 4)), np.tile(SL, (1, 4))], axis=1)
    c['mit8'] = np.concatenate([np.tile(I <= J, (1, 4)), np.tile(I >= J, (1, 4))], axis=1)
    names = [n for n in c if n not in ('ms8', 'mst8', 'mit8')] + ['ms8', 'mst8', 'mit8']
    offs = {}
    o = 0
    for n in names:
        offs[n] = (o, c[n].shape[1])
        o += c[n].shape[1]
    arr = np.concatenate([c[n].astype(np.float32) for n in names], axis=1)
    return arr, offs


def build(cfg):
    L, TS, TP, PAST = cfg['L'], cfg['TS'], cfg['TP'], cfg['PAST']
    NT = TS + 2 * TP
    dbg = cfg.get('dbg', None)
    nc = bass.Bass("TRN2", target_bir_lowering=False)
    lay, WTOT = wstream_layout()
    carr, coffs = make_consts()
    NCST = carr.shape[1]

    def din(name, shape, dt=F32):
        return nc.dram_tensor(name, list(shape), dt, kind="ExternalInput").ap()

    def dout(name, shape, dt=F32):
        return nc.dram_tensor(name, list(shape), dt, kind="ExternalOutput").ap()

    def dscr(name, shape, dt=F32):
        return nc.dram_tensor(name, list(shape), dt, kind="Internal").ap()
    xT_in = din('xT', [128, 8, NT])
    wst = din('wst', [L, 128, WTOT])
    cst_in = din('cst', [128, NCST])
    condT_in = din('condT', [128, 8, 2])
    bmodT_in = din('bmodT', [L, 128, 72])
    normw_in = din('normwT', [L, 128, 24])
    rope_in = din('rope', [TS, 64])
    qkw_in = din('qkw', [L, 640])
    onw_in = din('onw', [L, 256])
    dnvec_in = din('dnvec', [L, 16])
    dncw_in = din('dncw', [L, 128, 24])
    lruc_in = din('lruc', [L, 128, 2, 16])
    lruw_in = din('lruw', [L, 128, 8, 128])
    ck_in = din('cachek', [L, PAST, 128])
    cv_in = din('cachev', [L, PAST, 128])
    sd_in = din('sd0', [L, 128, 256])
    yT = dout('yT', [128, 8, NT])
    ok_out = dout('newk', [L, 2 * TP, 128])
    ov_out = dout('newv', [L, 2 * TP, 128])
    osd_out = dout('newsd', [L, 2, 128, 256])
    osl_out = dout('newsl', [L, 2, 128, 4])
    fm_raw = dscr('fm_raw', [128, 10, NT])
    tm_raw = dscr('tm_raw', [NT, 1040])
    dnq = dscr('dnq', [128, 6, NT], BF16)
    obuf_d = dscr('obuf_d', [128, 8, NT], BF16)
    wbf = [dscr('wbf%d' % l, [128, WTOT], BF16) for l in range(L)]
    dbg_out = dout('dbg', [128, 8, NT]) if dbg else None

    P = Prog(nc)
    sb_off = [16384 + 256]
    sb_max = [0]

    def sb(name, shape, dt=F32):
        esz = 4 if dt == F32 else 2
        n = int(np.prod(shape[1:])) * esz
        n = (n + 31) // 32 * 32
        h = nc.alloc_sbuf_tensor_at(name, list(shape), dt, offset=sb_off[0])
        sb_off[0] += n
        sb_max[0] = max(sb_max[0], sb_off[0])
        assert sb_off[0] <= 222 * 1024, (name, sb_off[0])
        return Buf(name, h)
    psb = []
    for i in range(8):
        psb.append(Buf('ps%d' % i, nc.alloc_psum_tensor('ps%d' % i, [128, 512], F32)))
        psb[-1].psum = True
    PS = FreeList(psb)
    NCP = coffs['ms8'][0]
    cst = sb('cst', [128, NCP])
    cst_bf = sb('cst_bf', [128, 256], BF16)
    coef = sb('coef', [128, L, 3, 3, 8, 2])
    modT = sb('modT', [128, 72, 2])
    smallc = sb('smallc', [128, 64])
    ring = [sb('ring%d' % i, [128, 4096], BF16) for i in range(5)]
    ring_i = [0]
    PERM_END = sb_off[0]

    def C(name):
        o, n = coffs[name]
        return cst.h[:, o:o + n]
    ident_f = C('ident')
    ones_f = C('ones')
    ident_b = cst_bf.h[:, 0:128]
    ones_b = cst_bf.h[:, 128:256]

    def mm(out, lhsT, rhs, start, stop, R, W):
        P.op('pe', lambda e: e.matmul(out, lhsT, rhs, start=start, stop=stop), R, W)

    def tr(out, in_, R, W):
        P.op('pe', lambda e: e.transpose(out, in_, ident_f), R + [cst], W)

    def act(out, in_, func, R, W, bias=None, scale=1.0):
        if bias is None:
            P.op('act', lambda e: e.activation(out=out, in_=in_, func=func, scale=scale), R, W)
        else:
            P.op('act', lambda e: e.activation(out=out, in_=in_, func=func, bias=bias, scale=scale), list(R) + [smallc], W)

    def tt(eng, out, in0, in1, op, R, W):
        P.op(eng, lambda e: e.tensor_tensor(out=out, in0=in0, in1=in1, op=op), R, W)

    def ts(eng, out, in0, s1, s2, op0, op1, R, W):
        if s2 is None:
            P.op(eng, lambda e: e.tensor_scalar(out=out, in0=in0, scalar1=s1, scalar2=None, op0=op0), R, W)
        else:
            P.op(eng, lambda e: e.tensor_scalar(out=out, in0=in0, scalar1=s1, scalar2=s2, op0=op0, op1=op1), R, W)

    def stt(out, in0, scalar, in1, op0, op1, R, W):
        P.op('dve', lambda e: e.scalar_tensor_tensor(out=out, in0=in0, scalar=scalar, in1=in1, op0=op0, op1=op1), R, W)

    def cp(eng, out, in_, R, W):
        if eng == 'act':
            act(out, in_, AF.Copy, R, W)
        else:
            P.op(eng, lambda e: e.tensor_copy(out=out, in_=in_), R, W)

    def recip(out, in_, R, W):
        P.op('dve', lambda e: e.reciprocal(out=out, in_=in_), R, W)

    def memset(eng, out, val, W):
        P.op(eng, lambda e: e.memset(out, val), [], W)

    def ld(out, in_, W, anchor=None, R=()):
        P.dma('sp', out, in_, list(R), [W], anchor or W)

    def st(out, in_, Rb, Wd=(), anchor=None, q='sp'):
        P.dma(q, out, in_, [Rb], list(Wd), anchor or Rb)

    def wload(l, name):
        o, n = lay[name]
        slot = ring[ring_i[0] % 5]
        ring_i[0] += 1
        slot.gen += 1
        P.dma('sp', slot.h[:, 0:n], wbf[l][:, o:o + n], [wbfM if name.startswith('mod') else wbfL[l]], [slot], slot)
        return slot, n

    wbfL = [Buf('wbfL%d' % l, None) for l in range(L)]
    CW = 8192

    wbfM = Buf('wbfM', None)
    MOD0 = lay['mod_0'][0]

    def cast_weights(l):
        for c0 in range(0, MOD0, CW):
            c1 = min(MOD0, c0 + CW)
            P.dma('pool', wbf[l][:, c0:c1], wst[l, :, c0:c1], [], [wbfL[l]], wbfL[l], serialize=False, max_dma_last_dim=8192)
    for l in range(L):
        for c0 in range(MOD0, WTOT, CW):
            c1 = min(WTOT, c0 + CW)
            P.dma('pool', wbf[l][:, c0:c1], wst[l, :, c0:c1], [], [wbfM], wbfM, serialize=False, max_dma_last_dim=8192)
    cast_weights(0)

    ld(cst.h[:, :], cst_in[:, 0:NCP], cst)
    cp('dve', cst_bf.h[:, 0:128], ident_f, [cst], [cst_bf])
    cp('dve', cst_bf.h[:, 128:256], ones_f, [cst], [cst_bf])
    sb_off[0] = PERM_END
    condt = sb('condt', [128, 8, 2])
    condb = sb('condb', [128, 8, 2], BF16)
    sgc = sb('sgc', [128, 8, 2])
    bmt = sb('bmt', [128, 72])
    nwt = sb('nwt', [128, 24])
    ld(condt.h[:], condT_in[:], condt, anchor=cst)
    act(sgc.h[:], condt.h[:], AF.Sigmoid, [condt], [sgc])
    tt('dve', condb.h[:], condt.h[:], sgc.h[:], ALU.mult, [condt, sgc], [condb])
    for l in range(L):
        ld(bmt.h[:], bmodT_in[l], bmt, anchor=cst)
        ld(nwt.h[:], normw_in[l], nwt, anchor=cst)
        psM = PS.get()
        for blk in range(18):
            slot, n = wload(l, 'mod_%d' % blk)
            for jj in range(4):
                j = blk * 4 + jj
                for k in range(8):
                    mm(psM.h[:, 2 * j:2 * j + 2], slot.h[:, k * 512 + jj * 128:k * 512 + jj * 128 + 128],
                       condb.h[:, k, :], k == 0, k == 7, [slot, condb], [psM])
        tt('dve', modT.h[:], psM.h[:, 0:144].rearrange('p (j c) -> p j c', c=2),
           bmt.h[:].unsqueeze(2).broadcast_to([128, 72, 2]), ALU.add, [psM, bmt], [modT])
        PS.put(psM)
        mv = modT.h[:].rearrange('p (s m k) c -> p s m k c', s=3, m=3)
        ts('dve', coef.h[:, l, 0], mv[:, :, 1], 1.0, None, ALU.add, None, [modT], [coef])
        tt('dve', coef.h[:, l, 0], coef.h[:, l, 0],
           nwt.h[:].rearrange('p (s k) -> p s k', s=3).unsqueeze(3).broadcast_to([128, 3, 8, 2]), ALU.mult,
           [coef, nwt], [coef])
        cp('dve', coef.h[:, l, 1], mv[:, :, 0], [modT], [coef])
        ts('dve', coef.h[:, l, 2], mv[:, :, 2], 0.5, None, ALU.mult, None, [modT], [coef])
        ts('dve', coef.h[:, l, 2, 1], mv[:, 1, 2], 1.0, None, ALU.mult, None, [modT], [coef])
    P.barrier()

    sb_off[0] = PERM_END
    xsb = [sb('xs%d' % i, [128, 8, 512]) for i in range(2)]
    xs = xsb[0]
    cur = {'xs': xsb[0], 'pre': None, 'pend': None}
    xsst = [Buf('xsst0', None), Buf('xsst1', None)]
    sq = sb('sq', [128, 8, 512], BF16)
    rt = sb('rt', [128, 512])
    tmpA = [sb('tmpA%d' % i, [128, 512]) for i in range(2)]
    hb = sb('hb', [128, 8, 512], BF16)
    actb = sb('actb', [128, 22, 512], BF16)
    sgt = [sb('sgt%d' % i, [128, 512]) for i in range(2)]
    fmst = sb('fmst', [128, 10, 512])
    tmst = sb('tmst', [128, 4, 1040])
    merged = sb('merged', [128, 8, 512])
    mergedb = sb('mergedb', [128, 8, 512], BF16)
    gtb = [sb('gtb%d' % i, [128, 512]) for i in range(2)]
    obt = sb('obt', [128, 8, 512], BF16)
    AC_END = sb_off[0]
    cnt = [0]

    tiles = [(t0, 0) for t0 in range(0, TS, 512)] + [(TS, 1)]

    def sq_flush():
        mo = cur['pend']
        if mo is None:
            return
        mm(cur['pre'].h[:, :], ones_b, sq.h[:, mo, :], mo == 0, mo == 7, [cst_bf, sq], [cur['pre']])
        cur['pend'] = None

    def sq_accum(mo):
        xs = cur['xs']
        if mo == 0:
            cur['pre'] = PS.get()
        act(sq.h[:, mo, :], xs.h[:, mo, :], AF.Square, [xs], [sq])
        cur['pend'] = mo

    def norm_mod(l, s, c):
        xs = cur['xs']
        if cur['pre'] is not None:
            sq_flush()
            ps = cur['pre']
            cur['pre'] = None
        else:
            act(sq.h[:], xs.h[:], AF.Square, [xs], [sq])
            ps = PS.get()
            for k in range(8):
                mm(ps.h[:, :], ones_b, sq.h[:, k, :], k == 0, k == 7, [cst_bf, sq], [ps])
        act(rt.h[:], ps.h[:], AF.Sqrt, [ps], [rt], bias=smallc.h[:, 0:1], scale=1.0 / D)
        PS.put(ps)
        recip(rt.h[:], rt.h[:], [rt], [rt])
        for k in range(8):
            t = tmpA[k % 2]
            stt(t.h[:], xs.h[:, k, :], coef.h[:, l, 0, s, k, c:c + 1], rt.h[:], ALU.mult, ALU.mult, [xs, coef, rt], [t])
            act(hb.h[:, k, :], t.h[:], AF.Identity, [t, coef], [hb], bias=coef.h[:, l, 1, s, k, c:c + 1])

    def ffn(l, f, c, accum_next=True):
        s = 0 if f == 1 else 2
        xs = cur['xs']
        norm_mod(l, s, c)
        for b in range(11):
            slot, n = wload(l, 'gu%d_%d' % (f, b))
            for m2 in range(2):
                m = 2 * b + m2
                psg = PS.get()
                psu = PS.get()
                for k in range(8):
                    mm(psg.h[:, :], slot.h[:, k * 512 + m2 * 256:k * 512 + m2 * 256 + 128], hb.h[:, k, :], k == 0, k == 7, [slot, hb], [psg])
                for k in range(8):
                    mm(psu.h[:, :], slot.h[:, k * 512 + m2 * 256 + 128:k * 512 + m2 * 256 + 256], hb.h[:, k, :], k == 0, k == 7, [slot, hb], [psu])
                sg = sgt[m % 2]
                act(sg.h[:], psg.h[:], AF.Silu, [psg], [sg])
                PS.put(psg)
                tt('dve', actb.h[:, m, :], sg.h[:], psu.h[:], ALU.mult, [sg, psu], [actb])
                PS.put(psu)
        for mo in range(8):
            slot, n = wload(l, 'wd%d_%d' % (f, mo))
            ps = PS.get()
            for kc in range(22):
                mm(ps.h[:, :], slot.h[:, kc * 128:(kc + 1) * 128], actb.h[:, kc, :], kc == 0, kc == 21, [slot, actb], [ps])
            sq_flush()
            stt(xs.h[:, mo, :], ps.h[:], coef.h[:, l, 2, s, mo, c:c + 1], xs.h[:, mo, :], ALU.mult, ALU.add, [ps, coef, xs], [xs])
            PS.put(ps)
            if accum_next:
                sq_accum(mo)

    def win_proj(l, t0, c):
        norm_mod(l, 1, c)
        j = 0
        for i, ncol in enumerate((512, 512, 256)):
            slot, n = wload(l, 'infm_%d' % i)
            for jj in range(ncol // 128):
                ps = PS.get()
                for k in range(8):
                    mm(ps.h[:, :], slot.h[:, k * ncol + jj * 128:k * ncol + jj * 128 + 128], hb.h[:, k, :], k == 0, k == 7, [slot, hb], [ps])
                cp('act' if j % 2 else 'dve', fmst.h[:, j, :], ps.h[:], [ps], [fmst])
                PS.put(ps)
                j += 1
        st(fm_raw[:, :, t0:t0 + 512], fmst.h[:], fmst, q='pool')
        co = 0
        for i, ncol in enumerate((272, 512, 256)):
            slot, n = wload(l, 'intm_%d' % i)
            for tsub in range(4):
                ps = PS.get()
                for k in range(8):
                    mm(ps.h[:, 0:ncol], hb.h[:, k, tsub * 128:(tsub + 1) * 128], slot.h[:, k * ncol:(k + 1) * ncol], k == 0, k == 7, [slot, hb], [ps])
                cp('act' if tsub % 2 else 'dve', tmst.h[:, tsub, co:co + ncol], ps.h[:, 0:ncol], [ps], [tmst])
                PS.put(ps)
            co += ncol
        st(tm_raw[t0:t0 + 512, :].rearrange('(t p) c -> p t c', p=128), tmst.h[:], tmst, q='pool')

    def mix_out(l, t0, c):
        xs = cur['xs']
        norm_mod(l, 1, c)
        ld(obt.h[:], obuf_d[:, :, t0:t0 + 512], obt)
        chunks = {0: [0, 1], 1: [2, 3, 4, 5], 2: [6, 7]}
        pname = {0: 'pa', 1: 'pb', 2: 'pc'}
        for b in range(3):
            slotp, _ = wload(l, pname[b])
            nk = len(chunks[b])
            for hf in range(2):
                slotg, _ = wload(l, 'gate_%d_%d' % (b, hf))
                for jj in range(4):
                    j = hf * 4 + jj
                    ps1 = PS.get()
                    for k in range(8):
                        mm(ps1.h[:, :], slotg.h[:, k * 512 + jj * 128:k * 512 + jj * 128 + 128], hb.h[:, k, :], k == 0, k == 7, [slotg, hb], [ps1])
                    g = gtb[j % 2]
                    act(g.h[:], ps1.h[:], AF.Sigmoid, [ps1], [g])
                    PS.put(ps1)
                    ps2 = PS.get()
                    for kc in range(nk):
                        mm(ps2.h[:, :], slotp.h[:, kc * 1024 + j * 128:kc * 1024 + j * 128 + 128],
                           obt.h[:, chunks[b][kc], :], kc == 0, kc == nk - 1, [slotp, obt], [ps2])
                    if b == 0:
                        tt('dve', merged.h[:, j, :], g.h[:], ps2.h[:], ALU.mult, [g, ps2], [merged])
                    else:
                        tt('dve', g.h[:], g.h[:], ps2.h[:], ALU.mult, [g, ps2], [g])
                        if b == 1:
                            tt('pool', merged.h[:, j, :], merged.h[:, j, :], g.h[:], ALU.add, [merged, g], [merged])
                        else:
                            tt('pool', mergedb.h[:, j, :], merged.h[:, j, :], g.h[:], ALU.add, [merged, g], [mergedb])
                    PS.put(ps2)
        for hf in range(2):
            slot, _ = wload(l, 'wo_%d' % hf)
            for jj in range(4):
                j = hf * 4 + jj
                ps = PS.get()
                for k in range(8):
                    mm(ps.h[:, :], slot.h[:, k * 512 + jj * 128:k * 512 + jj * 128 + 128], mergedb.h[:, k, :], k == 0, k == 7, [slot, mergedb], [ps])
                sq_flush()
                stt(xs.h[:, j, :], ps.h[:], coef.h[:, l, 2, 1, j, c:c + 1], xs.h[:, j, :], ALU.mult, ALU.add, [ps, coef, xs], [xs])
                PS.put(ps)
                sq_accum(j)

    memset('dve', smallc.h[:, 0:1], EPS, [smallc])
    memset('dve', smallc.h[:, 1:2], 1.0, [smallc])

    def phaseAC(l_c, l_a):
        for ti, (t0, c) in enumerate(tiles):
            src = xT_in if l_c is None else yT
            xs = xsb[ti % 2]
            cur['xs'] = xs
            cur['pre'] = None
            cur['pend'] = None
            if ti == 0:
                ld(xs.h[:], src[:, :, t0:t0 + 512], xs)
            if ti + 1 < len(tiles):
                tn0 = tiles[ti + 1][0]
                ld(xsb[(ti + 1) % 2].h[:], src[:, :, tn0:tn0 + 512], xsb[(ti + 1) % 2])
            if l_c is not None:
                mix_out(l_c, t0, c)
                ffn(l_c, 2, c, accum_next=(l_a is not None))
            if l_a is not None:
                ffn(l_a, 1, c, accum_next=True)
                win_proj(l_a, t0, c)
            st(yT[:, :, t0:t0 + 512], xs.h[:], xs, q='pool', anchor=xsst[ti % 2])
        P.barrier()

    seqs = [(0, TS, 0), (TS, TP, 1), (TS + TP, TP, 2)]

    def phaseB(l):
        sb_off[0] = PERM_END
        lru(l)
        P.barrier()
        sb_off[0] = PERM_END
        attention(l)
        P.barrier()
        sb_off[0] = PERM_END
        deltanet(l)
        P.barrier()

    def lru(l):
        TM = max(TS, TP)
        lc = sb('lc', [128, 2, 16])
        lw = sb('lw', [128, 8, 128])
        sp = sb('sp', [128, 2, 4])
        B0 = sb('B0', [128, TM + 3])
        B1 = sb('B1', [128, TM])
        B2 = sb('B2', [128, TM])
        B3 = sb('B3', [128, TM])
        B4 = sb('B4', [128, TM])
        B5 = sb('B5', [128, TM])
        Bo = sb('Bo', [128, TM], BF16)
        slst = sb('slst', [128, 2, 4])
        ld(lc.h[:], lruc_in[l], lc)
        ld(lw.h[:], lruw_in[l], lw, anchor=lc)
        for cc in range(2):
            act(sp.h[:, cc, 0:2], lc.h[:, cc, 9:11], AF.Exp, [lc], [sp], scale=-1.0)
            act(sp.h[:, cc, 0:2], sp.h[:, cc, 0:2], AF.Ln, [sp], [sp], bias=smallc.h[:, 1:2])
            ts('dve', sp.h[:, cc, 2:4], sp.h[:, cc, 0:2], -16.0, None, ALU.mult, None, [sp], [sp])
            ts('dve', sp.h[:, cc, 0:2], sp.h[:, cc, 0:2], -8.0, None, ALU.mult, None, [sp], [sp])
        for (s0, T, si) in seqs:
            for cc in range(2):
                memset('dve', B0.h[:, 0:2], 0.0, [B0])
                memset('dve', B0.h[:, T + 2:T + 3], 0.0, [B0])
                ld(B0.h[:, 2:T + 2], fm_raw[:, 6 + cc, s0:s0 + T], B0)
                ts('dve', B1.h[:, 0:T], B0.h[:, 0:T], lc.h[:, cc, 0:1], lc.h[:, cc, 4:5], ALU.mult, ALU.add, [B0, lc], [B1])
                for j in range(1, 4):
                    stt(B1.h[:, 0:T], B0.h[:, j:j + T], lc.h[:, cc, j:j + 1], B1.h[:, 0:T], ALU.mult, ALU.add, [B0, lc, B1], [B1])
                for d in range(2):
                    for gi, dst in ((0, B2), (1, B4)):
                        for ct in range(0, T, 512):
                            w_ = min(512, T - ct)
                            ps = PS.get()
                            mm(ps.h[:, 0:w_], lw.h[:, gi * 4 + d * 2 + cc, :], B1.h[:, ct:ct + w_], True, True, [lw, B1], [ps])
                            act(dst.h[:, ct:ct + w_], ps.h[:, 0:w_], AF.Sigmoid, [ps, lc], [dst], bias=lc.h[:, cc, 5 + gi * 2 + d:6 + gi * 2 + d])
                            PS.put(ps)
                    act(B3.h[:, 0:T], B2.h[:, 0:T], AF.Exp, [B2, sp], [B3], scale=sp.h[:, cc, 2 + d:3 + d])
                    act(B2.h[:, 0:T], B2.h[:, 0:T], AF.Exp, [B2, sp], [B2], scale=sp.h[:, cc, d:d + 1])
                    ts('dve', B3.h[:, 0:T], B3.h[:, 0:T], -1.0, 1.0, ALU.mult, ALU.add, [B3], [B3])
                    act(B3.h[:, 0:T], B3.h[:, 0:T], AF.Sqrt, [B3], [B3])
                    tt('dve', B3.h[:, 0:T], B3.h[:, 0:T], B4.h[:, 0:T], ALU.mult, [B3, B4], [B3])
                    tt('dve', B3.h[:, 0:T], B3.h[:, 0:T], B1.h[:, 0:T], ALU.mult, [B3, B1], [B3])
                    h0 = lc.h[:, cc, 11 + d:12 + d] if si == 0 else 0.0
                    dst = B5 if d == 0 else B4
                    if d == 0:
                        P.op('dve', (lambda e, o=dst.h[:, 0:T], a=B2.h[:, 0:T], u=B3.h[:, 0:T], h0=h0:
                                     e.tensor_tensor_scan(out=o, data0=a, data1=u, initial=h0, op0=ALU.mult, op1=ALU.add)),
                             [B2, B3, lc], [dst])
                    else:
                        P.op('dve', (lambda e, o=dst.h[:, 0:T][:, ::-1], a=B2.h[:, 0:T][:, ::-1], u=B3.h[:, 0:T][:, ::-1], h0=h0:
                                     e.tensor_tensor_scan(out=o, data0=a, data1=u, initial=h0, op0=ALU.mult, op1=ALU.add)),
                             [B2, B3, lc], [dst])
                if si > 0:
                    cp('dve', slst.h[:, si - 1, cc:cc + 1], B5.h[:, T - 1:T], [B5], [slst])
                    cp('dve', slst.h[:, si - 1, 2 + cc:3 + cc], B4.h[:, 0:1], [B4], [slst])
                    if cc == 1:
                        st(osl_out[l, si - 1], slst.h[:, si - 1, :], slst, anchor=Bo)
                tt('dve', B5.h[:, 0:T], B5.h[:, 0:T], B4.h[:, 0:T], ALU.add, [B5, B4], [B5])
                ld(B0.h[:, 0:T], fm_raw[:, 8 + cc, s0:s0 + T], B0)
                tt('dve', B2.h[:, 0:T], B0.h[:, 0:T], B0.h[:, 0:T], ALU.mult, [B0], [B2])
                ts('dve', B2.h[:, 0:T], B2.h[:, 0:T], 0.044715, 1.0, ALU.mult, ALU.add, [B2], [B2])
                tt('dve', B2.h[:, 0:T], B2.h[:, 0:T], B0.h[:, 0:T], ALU.mult, [B2, B0], [B2])
                act(B2.h[:, 0:T], B2.h[:, 0:T], AF.Sigmoid, [B2], [B2], scale=1.5957691216057308)
                tt('dve', B2.h[:, 0:T], B2.h[:, 0:T], B0.h[:, 0:T], ALU.mult, [B2, B0], [B2])
                tt('dve', Bo.h[:, 0:T], B2.h[:, 0:T], B5.h[:, 0:T], ALU.mult, [B2, B5], [Bo])
                st(obuf_d[:, 6 + cc, s0:s0 + T], Bo.h[:, 0:T], Bo)

    def attention(l):
        NKS = PAST + TS
        kT = sb('kT', [128, NKS], BF16)
        NCK = NKS // 128
        vaug = sb('vaug', [128, NCK, 2, 128], BF16)
        qT = sb('qT', [128, 2, 4, TS], BF16)
        wqk = sb('wqk', [128, 640])
        rin = [sb('rin%d' % i, [128, 768]) for i in range(2)]
        rp = [sb('rp%d' % i, [128, 64]) for i in range(2)]
        sqq = sb('sqq', [128, 640])
        ssq = sb('ssq', [128, 10])
        qn = sb('qn', [128, 640])
        qr = sb('qr', [128, 640])
        t1 = sb('t1', [128, 320])
        t2 = sb('t2', [128, 320])
        pT = [sb('pT%d' % i, [128, 512], BF16) for i in range(5)]
        rr = sb('rr', [128, 512])
        bc = sb('bc', [128, 512])
        obst = [sb('obst%d' % i, [128, 4, 128], BF16) for i in range(2)]
        ld(wqk.h[:], qkw_in[l:l + 1, :].broadcast_to([128, 640]), wqk, anchor=rin[0])
        memset('pool', qT.h[64:128, 0], 0.0, [qT])
        memset('pool', qT.h[0:64, 1], 0.0, [qT])
        memset('pool', vaug.h[:, :, 0, 64:128], 1.0, [vaug])
        memset('pool', vaug.h[:, :, 1, 0:64], 1.0, [vaug])
        cntl = [0]
        for (s0, T, si) in seqs:
            koff = PAST if si == 0 else 0
            nq = T // 128
            if si == 0:
                for ck in range(PAST // 128):
                    r = rin[cntl[0] % 2]
                    cntl[0] += 1
                    ld(r.h[:, 0:128], ck_in[l, ck * 128:(ck + 1) * 128, :], r)
                    ld(r.h[:, 128:256], cv_in[l, ck * 128:(ck + 1) * 128, :], r)
                    ps = PS.get()
                    tr(ps.h[:, 0:128], r.h[:, 0:128], [r], [ps])
                    cp('act', kT.h[:, ck * 128:(ck + 1) * 128], ps.h[:, 0:128], [ps], [kT])
                    PS.put(ps)
                    cp('dve', vaug.h[:, ck, 0, 0:64], r.h[:, 128:192], [r], [vaug])
                    cp('dve', vaug.h[:, ck, 1, 64:128], r.h[:, 192:256], [r], [vaug])
            for cq in range(nq):
                r = rin[cntl[0] % 2]
                rpp = rp[cntl[0] % 2]
                cntl[0] += 1
                tok = s0 + cq * 128
                ld(r.h[:], tm_raw[tok:tok + 128, 272:1040], r)
                if si == 0:
                    ld(rpp.h[:], rope_in[cq * 128:(cq + 1) * 128, :], rpp, anchor=r)
                act(sqq.h[:], r.h[:, 0:640], AF.Square, [r], [sqq])
                P.op('dve', lambda e, o=ssq.h[:], i=sqq.h[:].rearrange('p (h d) -> p h d', d=64): e.tensor_reduce(out=o, in_=i, axis=AX.X, op=ALU.add), [sqq], [ssq])
                act(ssq.h[:], ssq.h[:], AF.Sqrt, [ssq], [ssq], bias=smallc.h[:, 0:1], scale=1.0 / 64)
                recip(ssq.h[:], ssq.h[:], [ssq], [ssq])
                tt('dve', qn.h[:].rearrange('p (h d) -> p h d', d=64), r.h[:, 0:640].rearrange('p (h d) -> p h d', d=64),
                   ssq.h[:].unsqueeze(2).broadcast_to([128, 10, 64]), ALU.mult, [r, ssq], [qn])
                if si == 0:
                    tt('dve', qn.h[:], qn.h[:], wqk.h[:], ALU.mult, [qn, wqk], [qn])
                    qv = qn.h[:].rearrange('p (h a x r) -> p h a x r', h=10, a=2, x=2)
                    ov = qr.h[:].rearrange('p (h a x r) -> p h a x r', h=10, a=2, x=2)
                    cs = rpp.h[:, 0:32].rearrange('p (a r) -> p a r', a=2).unsqueeze(1).broadcast_to([128, 10, 2, 16])
                    sn = rpp.h[:, 32:64].rearrange('p (a r) -> p a r', a=2).unsqueeze(1).broadcast_to([128, 10, 2, 16])
                    t1v = t1.h[:].rearrange('p (h a r) -> p h a r', h=10, a=2)
                    t2v = t2.h[:].rearrange('p (h a r) -> p h a r', h=10, a=2)
                    tt('dve', t1v, qv[:, :, :, 0, :], cs, ALU.mult, [qn, rpp], [t1])
                    tt('pool', t2v, qv[:, :, :, 1, :], sn, ALU.mult, [qn, rpp], [t2])
                    tt('dve', ov[:, :, :, 0, :], t1v, t2v, ALU.subtract, [t1, t2], [qr])
                    tt('dve', t1v, qv[:, :, :, 1, :], cs, ALU.mult, [qn, rpp], [t1])
                    tt('pool', t2v, qv[:, :, :, 0, :], sn, ALU.mult, [qn, rpp], [t2])
                    tt('dve', ov[:, :, :, 1, :], t1v, t2v, ALU.add, [t1, t2], [qr])
                else:
                    tt('dve', qr.h[:], qn.h[:], wqk.h[:], ALU.mult, [qn, wqk], [qr])
                    pt = (si - 1) * TP + cq * 128
                    st(ok_out[l, pt:pt + 128, :], qr.h[:, 512:640], qr)
                    st(ov_out[l, pt:pt + 128, :], r.h[:, 640:768], r)
                qrv = qr.h[:].rearrange('p (h d) -> p h d', d=64)
                for g in range(4):
                    ps = PS.get()
                    tr(ps.h[:, 0:128], qr.h[:, g * 128:(g + 1) * 128], [qr], [ps])
                    cp('act' if g % 2 else 'dve', qT.h[0:64, 0, g, cq * 128:(cq + 1) * 128], ps.h[0:64, 0:128], [ps], [qT])
                    cp('act' if g % 2 else 'dve', qT.h[64:128, 1, g, cq * 128:(cq + 1) * 128], ps.h[64:128, 0:128], [ps], [qT])
                    PS.put(ps)
                ps = PS.get()
                tr(ps.h[:, 0:128], qr.h[:, 512:640], [qr], [ps])
                kc0 = koff + cq * 128
                cp('act', kT.h[:, kc0:kc0 + 128], ps.h[:, 0:128], [ps], [kT])
                PS.put(ps)
                ckk = kc0 // 128
                cp('dve', vaug.h[:, ckk, 0, 0:64], r.h[:, 640:704], [r], [vaug])
                cp('pool', vaug.h[:, ckk, 1, 64:128], r.h[:, 704:768], [r], [vaug])
            nck = (koff + T) // 128
            pi = 0
            for qb in range(nq):
                for kv in range(2):
                    psO = PS.get()
                    pend = []

                    def pv(ck, p):
                        mm(psO.h[:, :], vaug.h[:, ck, kv, :], p.h[:], ck == 0, ck == nck - 1, [vaug, p], [psO])
                    for ck in range(nck):
                        psS = PS.get()
                        mm(psS.h[:, :], kT.h[:, ck * 128:(ck + 1) * 128],
                           qT.h[:, kv, :, qb * 128:(qb + 1) * 128], True, True, [kT, qT], [psS])
                        p = pT[pi % 5]
                        pi += 1
                        act(p.h[:], psS.h[:], AF.Exp, [psS], [p], scale=0.125)
                        PS.put(psS)
                        pend.append((ck, p))
                        if len(pend) > 2:
                            pv(*pend.pop(0))
                    while pend:
                        pv(*pend.pop(0))
                    p0 = 64 if kv == 0 else 0
                    o0 = 0 if kv == 0 else 64
                    recip(rr.h[p0:p0 + 1, :], psO.h[p0:p0 + 1, :], [psO], [rr])
                    psB = PS.get()
                    mm(psB.h[:, :], ones_f[p0:p0 + 1, :], rr.h[p0:p0 + 1, :], True, True, [cst, rr], [psB])
                    cp('act', bc.h[o0:o0 + 64, :], psB.h[o0:o0 + 64, :], [psB], [bc])
                    PS.put(psB)
                    ob = obst[qb % 2]
                    tt('dve', ob.h[o0:o0 + 64, :, :],
                       psO.h[o0:o0 + 64, :].rearrange('p (g t) -> p g t', g=4),
                       bc.h[o0:o0 + 64, :].rearrange('p (g t) -> p g t', g=4), ALU.mult, [psO, bc], [ob])
                    PS.put(psO)
                st(obuf_d[:, 2:6, s0 + qb * 128:s0 + (qb + 1) * 128], obst[qb % 2].h[:], obst[qb % 2])

    def deltanet(l):
        TM = max(TS, TP)
        cw = sb('cw', [128, 24])
        dv = sb('dv', [128, 16])
        nea = sb('nea', [128, 8])
        onw = sb('onw', [128, 256])
        R0 = sb('R0', [128, TM + 3])
        R1 = sb('R1', [128, TM])
        R2 = sb('R2', [128, TM])
        R1b = sb('R1b', [128, TM], BF16)
        ld(cw.h[:], dncw_in[l], cw)
        ld(dv.h[:], dnvec_in[l:l + 1, :].broadcast_to([128, 16]), dv, anchor=cw)
        ld(onw.h[:], onw_in[l:l + 1, :].broadcast_to([128, 256]), onw, anchor=cw)
        act(nea.h[:], dv.h[:, 0:8], AF.Exp, [dv], [nea])
        ts('dve', nea.h[:], nea.h[:], -1.0, None, ALU.mult, None, [nea], [nea])
        blk64 = C('blk64')
        for (s0, T, si) in seqs:
            for cq in range(6):
                memset('dve', R0.h[:, 0:2], 0.0, [R0])
                memset('dve', R0.h[:, T + 2:T + 3], 0.0, [R0])
                ld(R0.h[:, 2:T + 2], fm_raw[:, cq, s0:s0 + T], R0)
                ts('dve', R1.h[:, 0:T], R0.h[:, 0:T], cw.h[:, cq * 4:cq * 4 + 1], None, ALU.mult, None, [R0, cw], [R1])
                for j in range(1, 4):
                    stt(R1.h[:, 0:T], R0.h[:, j:j + T], cw.h[:, cq * 4 + j:cq * 4 + j + 1], R1.h[:, 0:T], ALU.mult, ALU.add, [R0, cw, R1], [R1])
                if cq >= 4:
                    act(R1b.h[:, 0:T], R1.h[:, 0:T], AF.Silu, [R1], [R1b])
                else:
                    act(R1.h[:, 0:T], R1.h[:, 0:T], AF.Silu, [R1], [R1])
                if cq < 4:
                    tt('pool', R2.h[:, 0:T], R1.h[:, 0:T], R1.h[:, 0:T], ALU.mult, [R1], [R2])
                    for ct in range(0, T, 512):
                        w_ = min(512, T - ct)
                        ps = PS.get()
                        mm(ps.h[:, 0:w_], blk64, R2.h[:, ct:ct + w_], True, True, [cst, R2], [ps])
                        act(R0.h[:, ct:ct + w_], ps.h[:, 0:w_], AF.Sqrt, [ps], [R0], bias=smallc.h[:, 0:1])
                        PS.put(ps)
                    recip(R0.h[:, 0:T], R0.h[:, 0:T], [R0], [R0])
                    if cq < 2:
                        stt(R1b.h[:, 0:T], R1.h[:, 0:T], 0.125, R0.h[:, 0:T], ALU.mult, ALU.mult, [R1, R0], [R1b])
                    else:
                        tt('dve', R1b.h[:, 0:T], R1.h[:, 0:T], R0.h[:, 0:T], ALU.mult, [R1, R0], [R1b])
                st(dnq[:, cq, s0:s0 + T], R1b.h[:, 0:T], R1b, anchor=R1b)
        P.barrier()
        import os
        DNS = int(os.environ.get('DNSTOP', '100'))
        if DNS == 0:
            return
        sb_off[0] = PERM_END
        cw = None
        S = sb('S', [128, 4, 64])
        dv2 = sb('dv2', [128, 16])
        nea = sb('nea2', [128, 8])
        onw = sb('onw2', [128, 256])
        ld(dv2.h[:], dnvec_in[l:l + 1, :].broadcast_to([128, 16]), dv2, anchor=S)
        ld(onw.h[:], onw_in[l:l + 1, :].broadcast_to([128, 256]), onw, anchor=S)
        act(nea.h[:], dv2.h[:, 0:8], AF.Exp, [dv2], [nea])
        ts('dve', nea.h[:], nea.h[:], -1.0, None, ALU.mult, None, [nea], [nea])
        NCH = TM // 128
        oacc = sb('oacc', [128, NCH, 256])
        qkfm = [[sb('qkfm%d_%d' % (i, d), [128, 4, 128], BF16) for d in range(2)] for i in range(2)]
        vfm = [[sb('vfm%d_%d' % (i, d), [128, 2, 128], BF16) for d in range(2)] for i in range(2)]
        ba = [[sb('ba%d_%d' % (i, d), [128, 16]) for d in range(2)] for i in range(2)]
        ktm = sb('ktm', [128, 8, 64])
        vtm = sb('vtm', [128, 8, 64])
        gcb = sb('gcb', [128, 16])
        gg = sb('gg', [128, 8])
        gct = sb('gct', [128, 16])
        sm = sb('sm', [128, 8, 8])
        gt2 = sb('gt2', [128, 4])
        dg = sb('dg', [128, 16, 128])
        big = FreeList([sb('big%d' % i, [128, 8, 128]) for i in range(7)])
        bigB = FreeList([sb('bigB%d' % i, [128, 8, 128], BF16) for i in range(10)])
        Sb = sb('Sb', [128, 4, 64], BF16)
        bv = sb('bv', [128, 8, 64], BF16)
        usb = sb('usb', [128, 8, 64])
        wsb = sb('wsb', [128, 8, 64], BF16)
        wkT = sb('wkT', [128, 8, 128], BF16)
        qdT = sb('qdT', [128, 8, 128], BF16)
        kz = sb('kz', [128, 8, 128], BF16)
        Bkz = sb('Bkz', [128, 8, 128], BF16)
        kdz = sb('kdz', [128, 8, 128], BF16)
        memset('pool', qdT.h[:], 0.0, [qdT])
        memset('pool', kz.h[:], 0.0, [kz])
        memset('pool', Bkz.h[:], 0.0, [Bkz])
        memset('pool', kdz.h[:], 0.0, [kdz])
        zt = sb('zt', [128, 256])
        od = sb('od', [128, 256])
        osq = sb('osq', [128, 256])
        oss = sb('oss', [128, 4])
        mk = sb('mk', [128, 3, 8, 128])
        ld(mk.h[:], cst_in[:, NCP:NCP + 3072].rearrange('p (m u c) -> p m u c', m=3, u=8), mk, anchor=S)
        ms8 = mk.h[:, 0]
        mst8 = mk.h[:, 1]
        mit8 = mk.h[:, 2]
        odst = [sb('odst%d' % i, [128, 2, 128], BF16) for i in range(2)]
        NCA = NT // 128
        BA = sb('BA', [128, NCA, 16])
        beta_all = sb('beta_all', [128, NCA, 8])
        g_all = sb('g_all', [128, NCA, 8])
        for c0 in range(0, NCA, 8):
            c1 = min(NCA, c0 + 8)
            ld(BA.h[:, c0:c1, :], tm_raw[c0 * 128:c1 * 128, 256:272].rearrange('(c p) k -> p c k', p=128), BA, anchor=S)
        act(beta_all.h[:], BA.h[:, :, 0:8], AF.Exp, [BA], [beta_all], scale=-1.0)
        ts('dve', beta_all.h[:], beta_all.h[:], 1.0, None, ALU.add, None, [beta_all], [beta_all])
        recip(beta_all.h[:], beta_all.h[:], [beta_all], [beta_all])
        tt('dve', g_all.h[:], BA.h[:, :, 8:16], dv2.h[:, 8:16].unsqueeze(1).broadcast_to([128, NCA, 8]), ALU.add, [BA, dv2], [g_all])
        act(g_all.h[:], g_all.h[:], AF.Exp, [g_all], [g_all])
        act(g_all.h[:], g_all.h[:], AF.Ln, [g_all], [g_all], bias=smallc.h[:, 1:2])
        tt('dve', g_all.h[:], g_all.h[:], nea.h[:].unsqueeze(1).broadcast_to([128, NCA, 8]), ALU.mult, [g_all, nea], [g_all])

        def bc8(name):
            return C(name).unsqueeze(1).broadcast_to([128, 8, 128])
        stepi = [0]

        def stage(lhsT_t, rhs_t, evac, ident=False):
            for gi in range(2):
                ps = PS.get()
                for uu in range(4):
                    u = gi * 4 + uu
                    if ident:
                        mm(ps.h[:, uu * 128:(uu + 1) * 128], lhsT_t.h[:, u, :], ident_b, True, True, [lhsT_t, cst_bf], [ps])
                    else:
                        mm(ps.h[:, uu * 128:(uu + 1) * 128], lhsT_t.h[:, u, :], rhs_t.h[:, u, :], True, True, [lhsT_t, rhs_t], [ps])
                evac(gi, ps)
                PS.put(ps)

        def g4(t, gi):
            return t.h[:, gi * 4:(gi + 1) * 4, :]

        def p4(ps):
            return ps.h[:, :].rearrange('p (u c) -> p u c', u=4)

        def ev_copy(dst, eng='act'):
            def f(gi, ps):
                cp(eng, g4(dst, gi), p4(ps), [ps], [dst])
            return f

        def ev_mask(dst, mname):
            mk4 = C(mname).unsqueeze(1).broadcast_to([128, 4, 128])

            def f(gi, ps):
                tt('dve', g4(dst, gi), p4(ps), mk4, ALU.mult, [ps, cst], [dst])
            return f

        def ev_add(dst, base, op):
            def f(gi, ps):
                tt('dve', g4(dst, gi), g4(base, gi), p4(ps), op, [ps, base], [dst])
            return f

        if DNS == 10:
            return
        for (s0, T, si) in seqs:
            N = T // 128
            if si == 0:
                ld(S.h[:], sd_in[l].rearrange('p (i v) -> p i v', i=4), S)
            else:
                memset('dve', S.h[:], 0.0, [S])
            cp('dve', Sb.h[:], S.h[:], [S], [Sb])
            seen = set()
            for n in range(N):
                par = stepi[0] % 2
                stepi[0] += 1
                cis = (n, N - 1 - n)
                for d in range(2):
                    tok = s0 + cis[d] * 128
                    ld(qkfm[par][d].h[:], dnq[:, 0:4, tok:tok + 128], qkfm[par][d], anchor=qkfm[par][0])
                    ld(vfm[par][d].h[:], dnq[:, 4:6, tok:tok + 128], vfm[par][d], anchor=qkfm[par][0])
                for d in range(2):
                    for c2 in range(2):
                        ps = PS.get()
                        mm(ps.h[:, 0:128], qkfm[par][d].h[:, 2 + c2, :], ident_b, True, True, [qkfm[par][d], cst_bf], [ps])
                        cp('act', ktm.h[:, d * 4 + c2 * 2:d * 4 + c2 * 2 + 2, :], ps.h[:, 0:128].rearrange('p (h c) -> p h c', h=2), [ps], [ktm])
                        PS.put(ps)
                        ps = PS.get()
                        mm(ps.h[:, 0:128], vfm[par][d].h[:, c2, :], ident_b, True, True, [vfm[par][d], cst_bf], [ps])
                        cp('dve', vtm.h[:, d * 4 + c2 * 2:d * 4 + c2 * 2 + 2, :], ps.h[:, 0:128].rearrange('p (h c) -> p h c', h=2), [ps], [vtm])
                        PS.put(ps)
                if DNS == 11:
                    return
                gci = [(s0 + cis[d] * 128) // 128 for d in range(2)]
                for d in range(2):
                    cp('pool', gcb.h[:, 8 + d * 4:12 + d * 4], beta_all.h[:, gci[d], d * 4:d * 4 + 4], [beta_all], [gcb])
                if DNS == 12:
                    return
                ps = PS.get()
                mm(ps.h[:, 0:4], C('UI'), g_all.h[:, gci[0], 0:4], True, True, [cst, g_all], [ps])
                mm(ps.h[:, 4:8], C('LI'), g_all.h[:, gci[1], 4:8], True, True, [cst, g_all], [ps])
                mm(ps.h[:, 8:12], ones_f, g_all.h[:, gci[0], 0:4], True, True, [cst, g_all], [ps])
                mm(ps.h[:, 12:16], ones_f, g_all.h[:, gci[1], 4:8], True, True, [cst, g_all], [ps])
                cp('dve', gct.h[:], ps.h[:, 0:16], [ps], [gct])
                PS.put(ps)
                cp('dve', gcb.h[:, 0:8], gct.h[:, 0:8], [gct], [gcb])
                act(sm.h[:, 0, :], gct.h[:, 0:8], AF.Exp, [gct], [sm])
                tt('dve', sm.h[:, 1, :], gct.h[:, 8:16], gct.h[:, 0:8], ALU.subtract, [gct], [sm])
                act(sm.h[:, 1, :], sm.h[:, 1, :], AF.Exp, [sm], [sm])
                act(sm.h[:, 2, :], gct.h[:, 8:16], AF.Exp, [gct], [sm])
                tt('dve', sm.h[:, 3, :], sm.h[:, 0, :], gcb.h[:, 8:16], ALU.mult, [sm, gcb], [sm])
                cp('dve', gt2.h[0:64, :], sm.h[0:64, 2, 0:8:2], [sm], [gt2])
                cp('dve', gt2.h[64:128, :], sm.h[64:128, 2, 1:8:2], [sm], [gt2])
                if DNS == 1:
                    return
                tt('dve', dg.h[:], ident_f.unsqueeze(1).broadcast_to([128, 16, 128]),
                   gcb.h[:].unsqueeze(2).broadcast_to([128, 16, 128]), ALU.mult, [cst, gcb], [dg])
                if DNS == 23:
                    return
                psR = [PS.get() for _ in range(4)]
                for q in range(4):
                    mm(psR[q].h[:, :], ones_f, dg.h[:, q * 4:(q + 1) * 4, :], True, True, [cst, dg], [psR[q]])
                if DNS == 24:
                    return
                df = big.get()
                egr = big.get()
                br = big.get()
                for gi in range(2):
                    tt('dve', g4(df, gi), gcb.h[:, gi * 4:gi * 4 + 4].unsqueeze(2).broadcast_to([128, 4, 128]), p4(psR[gi]), ALU.subtract, [gcb, psR[gi]], [df])
                    act(g4(egr, gi), p4(psR[gi]), AF.Exp, [psR[gi]], [egr])
                    cp('act', g4(br, gi), p4(psR[2 + gi]), [psR[2 + gi]], [br])
                for q in range(4):
                    PS.put(psR[q])
                if DNS == 20:
                    return
                e1 = big.get()
                e2 = big.get()
                ts('dve', e1.h[:], df.h[:], 0.0, None, ALU.min, None, [df], [e1])
                ts('dve', e2.h[:], df.h[:], 0.0, None, ALU.max, None, [df], [e2])
                big.put(df)
                act(e1.h[:], e1.h[:], AF.Exp, [e1], [e1])
                act(e2.h[:], e2.h[:], AF.Exp, [e2], [e2], scale=-1.0)
                tt('pool', e1.h[:], e1.h[:], ms8, ALU.mult, [e1, mk], [e1])
                dsT = big.get()
                tt('pool', dsT.h[:], e2.h[:], mst8, ALU.mult, [e2, mk], [dsT])
                tt('pool', e2.h[:], e2.h[:], mit8, ALU.mult, [e2, mk], [e2])
                tt('dve', dsT.h[:], dsT.h[:], br.h[:], ALU.mult, [dsT, br], [dsT])
                big.put(br)
                if DNS == 2:
                    return
                Nm = big.get()
                Gm = big.get()
                qkm = bigB.get()
                for d in range(2):
                    for pr in range(2):
                        pb = pr * 64
                        cp('pool', kz.h[pb:pb + 64, d * 4 + pr:d * 4 + pr + 3:2, :], qkfm[par][d].h[pb:pb + 64, 2:4, :], [qkfm[par][d]], [kz])
                psK = [PS.get() for _ in range(2)]
                psQ = [PS.get() for _ in range(2)]
                for u in range(8):
                    d, h = u // 4, u % 4
                    pb = (h % 2) * 64
                    mm(psK[d].h[:, h * 128:(h + 1) * 128], kz.h[:, u, :], qkfm[par][d].h[:, 2 + h // 2, :], True, True, [kz, qkfm[par][d]], [psK[d]])
                    mm(psQ[d].h[:, h * 128:(h + 1) * 128], kz.h[:, u, :], qkfm[par][d].h[:, h // 2, :], True, True, [kz, qkfm[par][d]], [psQ[d]])
                for gi in range(2):
                    tt('dve', g4(Nm, gi), p4(psK[gi]), g4(e1, gi), ALU.mult, [psK[gi], e1], [Nm])
                    tt('dve', g4(Gm, gi), p4(psK[gi]), g4(dsT, gi), ALU.mult, [psK[gi], dsT], [Gm])
                    tt('dve', g4(qkm, gi), p4(psQ[gi]), g4(e2, gi), ALU.mult, [psQ[gi], e2], [qkm])
                    PS.put(psK[gi])
                    PS.put(psQ[gi])
                big.put(e1)
                big.put(e2)
                big.put(dsT)
                if DNS == 22:
                    return
                NmB = bigB.get()
                tt('pool', NmB.h[:], Nm.h[:], gcb.h[:, 8:16].unsqueeze(2).broadcast_to([128, 8, 128]), ALU.mult, [Nm, gcb], [NmB])
                Nb = bigB.get()
                Gb = bigB.get()
                tt('pool', Gb.h[:], Gm.h[:], bc8('bd16'), ALU.mult, [Gm, cst], [Gb])
                tt('pool', Nb.h[:], NmB.h[:], bc8('bd16'), ALU.mult, [NmB, cst], [Nb])
                big.put(Nm)
                big.put(Gm)
                if DNS == 3:
                    return
                Tc = bigB.get()
                tt('dve', Tc.h[:], ident_f.unsqueeze(1).broadcast_to([128, 8, 128]), Gb.h[:], ALU.subtract, [cst, Gb], [Tc])
                G2 = bigB.get()
                N2 = bigB.get()
                stage(Nb, Gb, ev_copy(G2, 'act'))
                stage(Gb, Nb, ev_copy(N2, 'act'))
                bigB.put(Nb)
                bigB.put(Gb)
                Tn = bigB.get()
                stage(N2, Tc, ev_add(Tn, Tc, ALU.add))
                bigB.put(Tc)
                Tc = Tn
                G4 = bigB.get()
                N4 = bigB.get()
                stage(N2, G2, ev_copy(G4, 'act'))
                stage(G2, N2, ev_copy(N4, 'act'))
                bigB.put(G2)
                bigB.put(N2)
                Tn = bigB.get()
                stage(N4, Tc, ev_add(Tn, Tc, ALU.add))
                bigB.put(Tc)
                Tc = Tn
                N8 = bigB.get()
                stage(G4, N4, ev_copy(N8, 'act'))
                bigB.put(G4)
                bigB.put(N4)
                Tn = bigB.get()
                stage(N8, Tc, ev_add(Tn, Tc, ALU.add))
                bigB.put(Tc)
                bigB.put(N8)
                Tc = Tn
                if DNS == 4:
                    return
                for lv in range(3):
                    Tt = bigB.get()
                    Wm = bigB.get()
                    stage(Tc, None, ev_copy(Tt, 'act'), ident=True)
                    stage(NmB, Tc, ev_mask(Wm, ('off32', 'off64', 'off128')[lv]))
                    Tn = bigB.get()
                    stage(Tt, Wm, ev_add(Tn, Tc, ALU.subtract))
                    bigB.put(Tt)
                    bigB.put(Wm)
                    bigB.put(Tc)
                    Tc = Tn
                bigB.put(NmB)
                if DNS == 5:
                    return
                tt('dve', bv.h[:], vtm.h[:], gcb.h[:, 8:16].unsqueeze(2).broadcast_to([128, 8, 64]), ALU.mult, [vtm, gcb], [bv])
                for pr in range(2):
                    pb = pr * 64
                    tt('pool', Bkz.h[:, pr:8:2, pb:pb + 64], ktm.h[:, pr:8:2, :], sm.h[:, 3, pr:8:2].unsqueeze(2).broadcast_to([128, 4, 64]), ALU.mult, [ktm, sm], [Bkz])
                    tt('pool', kdz.h[:, pr:8:2, pb:pb + 64], ktm.h[:, pr:8:2, :], sm.h[:, 1, pr:8:2].unsqueeze(2).broadcast_to([128, 4, 64]), ALU.mult, [ktm, sm], [kdz])
                psU = PS.get()
                for u in range(8):
                    mm(psU.h[:, u * 64:(u + 1) * 64], Tc.h[:, u, :], bv.h[:, u, :], True, True, [Tc, bv], [psU])
                cp('act', usb.h[:], psU.h[:, :].rearrange('p (u c) -> p u c', u=8), [psU], [usb])
                PS.put(psU)
                for gi in range(2):
                    psWk = PS.get()
                    for uu in range(4):
                        u = gi * 4 + uu
                        mm(psWk.h[:, uu * 128:(uu + 1) * 128], Bkz.h[:, u, :], Tc.h[:, u, :], True, True, [Bkz, Tc], [psWk])
                    cp('dve', wkT.h[:, gi * 4:(gi + 1) * 4, :], p4(psWk), [psWk], [wkT])
                    PS.put(psWk)
                bigB.put(Tc)
                for d in range(2):
                    for pr in range(2):
                        pb = pr * 64
                        tt('dve', qdT.h[pb:pb + 64, d * 4 + pr:d * 4 + pr + 3:2, :], qkfm[par][d].h[pb:pb + 64, 0:2, :],
                           egr.h[pb:pb + 64, d * 4 + pr:d * 4 + pr + 3:2, :], ALU.mult, [qkfm[par][d], egr], [qdT])
                big.put(egr)
                if DNS == 6:
                    return
                psW = PS.get()
                for u in range(8):
                    d, h = u // 4, u % 4
                    pb = (h % 2) * 64
                    idx = d * 2 + h // 2
                    mm(psW.h[:, u * 64:(u + 1) * 64], wkT.h[:, u, :], Sb.h[:, idx, :], True, True, [wkT, Sb], [psW])
                tt('dve', wsb.h[:], usb.h[:], psW.h[:, :].rearrange('p (u c) -> p u c', u=8), ALU.subtract, [usb, psW], [wsb])
                PS.put(psW)
                psO = PS.get()
                psS = PS.get()
                for u in range(8):
                    d, h = u // 4, u % 4
                    pb = (h % 2) * 64
                    idx = d * 2 + h // 2
                    mm(psO.h[:, u * 64:(u + 1) * 64], qdT.h[:, u, :], Sb.h[:, idx, :], True, False, [qdT, Sb], [psO])
                    mm(psO.h[:, u * 64:(u + 1) * 64], qkm.h[:, u, :], wsb.h[:, u, :], False, True, [qkm, wsb], [psO])
                    mm(psS.h[:, u * 64:(u + 1) * 64], kdz.h[:, u, :], wsb.h[:, u, :], True, True, [kdz, wsb], [psS])
                bigB.put(qkm)
                for d in range(2):
                    ci = cis[d]
                    if ci in seen:
                        tt('dve', oacc.h[:, ci, :], oacc.h[:, ci, :], psO.h[:, d * 256:(d + 1) * 256], ALU.add, [oacc, psO], [oacc])
                    else:
                        cp('act', oacc.h[:, ci, :], psO.h[:, d * 256:(d + 1) * 256], [psO], [oacc])
                        seen.add(ci)
                PS.put(psO)
                tt('dve', S.h[:], S.h[:], gt2.h[:].unsqueeze(2).broadcast_to([128, 4, 64]), ALU.mult, [S, gt2], [S])
                for pr in range(2):
                    pb = pr * 64
                    tt('dve', S.h[pb:pb + 64, :, :], S.h[pb:pb + 64, :, :], psS.h[pb:pb + 64, :].rearrange('p (u v) -> p u v', u=8)[:, pr:8:2, :], ALU.add, [S, psS], [S])
                PS.put(psS)
                cp('act', Sb.h[:], S.h[:], [S], [Sb])
            if DNS == 7:
                return
            if si > 0:
                st(osd_out[l, si - 1], S.h[:].rearrange('p i v -> p (i v)'), S)
            for ci in range(N):
                tok = s0 + ci * 128
                ld(zt.h[:], tm_raw[tok:tok + 128, 0:256], zt, anchor=odst[0])
                act(osq.h[:], oacc.h[:, ci, :], AF.Square, [oacc], [osq])
                P.op('dve', lambda e, o=oss.h[:], i=osq.h[:].rearrange('p (h d) -> p h d', d=64): e.tensor_reduce(out=o, in_=i, axis=AX.X, op=ALU.add), [osq], [oss])
                act(oss.h[:], oss.h[:], AF.Sqrt, [oss], [oss], bias=smallc.h[:, 0:1], scale=1.0 / 64)
                recip(oss.h[:], oss.h[:], [oss], [oss])
                tt('dve', od.h[:].rearrange('p (h d) -> p h d', d=64), oacc.h[:, ci, :].rearrange('p (h d) -> p h d', d=64),
                   oss.h[:].unsqueeze(2).broadcast_to([128, 4, 64]), ALU.mult, [oacc, oss], [od])
                tt('dve', od.h[:], od.h[:], onw.h[:], ALU.mult, [od, onw], [od])
                act(osq.h[:], zt.h[:], AF.Silu, [zt], [osq])
                tt('dve', od.h[:], od.h[:], osq.h[:], ALU.mult, [od, osq], [od])
                osd_ = odst[ci % 2]
                for c2 in range(2):
                    ps = PS.get()
                    tr(ps.h[:, 0:128], od.h[:, c2 * 128:(c2 + 1) * 128], [od], [ps])
                    cp('act', osd_.h[:, c2, :], ps.h[:, 0:128], [ps], [osd_])
                    PS.put(ps)
                st(obuf_d[:, 0:2, tok:tok + 128], osd_.h[:], osd_, anchor=odst[0])

    import os
    stop = int(os.environ.get('KSTOP', '1000'))
    phases = [lambda: phaseAC(None, 0)]
    for l in range(L):
        phases.append(lambda l=l: ((cast_weights(l + 1) if l + 1 < L else None), sb_off.__setitem__(0, PERM_END), lru(l), P.barrier()))
        phases.append(lambda l=l: (sb_off.__setitem__(0, PERM_END), attention(l), P.barrier()))
        phases.append(lambda l=l: (sb_off.__setitem__(0, PERM_END), deltanet(l), P.barrier()))
        phases.append(lambda l=l: phaseAC(l, l + 1 if l + 1 < L else None))
    for i, ph in enumerate(phases):
        if i >= stop:
            break
        ph()
    P.barrier()
    print('ops', {e: len(P.ops[e]) for e in ENGS}, 'sems', len(P.sems), 'sbuf_max', sb_max[0])
    P.emit()
    return nc


def prep_inputs(inp, cfg):
    L, TS, TP, PAST = cfg['L'], cfg['TS'], cfg['TP'], cfg['PAST']
    f = lambda a: np.ascontiguousarray(np.asarray(a, dtype=np.float32))
    carr, _ = make_consts()
    wsts = np.stack([build_wstream(inp, l) for l in range(L)])
    bmodT = f(inp['b_mod'].reshape(L, 72, 128).transpose(0, 2, 1))
    normwT = f(inp['norm_w'].reshape(L, 3, 8, 128).transpose(0, 3, 1, 2).reshape(L, 128, 24))
    rows = TS // 64
    row = np.repeat(np.arange(rows, dtype=np.float32), 64)
    col = np.tile(np.arange(64, dtype=np.float32), rows)
    freqs = (np.float32(10000.0) ** (-np.arange(16, dtype=np.float32) / np.float32(16))).astype(np.float32)
    ang = np.stack([row[:, None] * freqs, col[:, None] * freqs], axis=1).astype(np.float32)
    rope = f(np.concatenate([np.cos(ang).reshape(TS, 32), np.sin(ang).reshape(TS, 32)], axis=1))
    qkw = f(np.concatenate([np.tile(inp['att_qnorm_w'], (1, 8)), np.tile(inp['att_knorm_w'], (1, 2))], axis=1))
    onw = f(np.tile(inp['dn_onorm_w'], (1, 4)))
    dnvec = f(np.concatenate([inp['dn_a_log'].reshape(L, 8), inp['dn_dt_bias'].reshape(L, 8)], axis=1))
    dncw = f(inp['dn_conv_w'].reshape(L, 4, 6, 128).transpose(0, 3, 2, 1).reshape(L, 128, 24))
    lruw = np.zeros((L, 128, 8, 128), np.float32)
    for gi, nm in enumerate(('lru_wr', 'lru_wi')):
        w = inp[nm]
        for d in range(2):
            for cc in range(2):
                for hb_ in range(2):
                    blk = cc * 2 + hb_
                    lruw[:, hb_ * 64:(hb_ + 1) * 64, gi * 4 + d * 2 + cc, hb_ * 64:(hb_ + 1) * 64] = w[:, d, blk]
    maps = []
    nb = inp['x_sample'].shape[0]
    for b in range(nb):
        xs = inp['x_sample'][b]
        xp = inp['x_prompt'][2 * b:2 * b + 2].reshape(2 * TP, D)
        x = np.concatenate([xs, xp], axis=0)
        xT = f(x.T.reshape(8, 128, -1).transpose(1, 0, 2))
        cond = np.stack([inp['c'][b], inp['c_ctx']], axis=1)
        condT = f(cond.reshape(8, 128, 2).transpose(1, 0, 2))
        lruc = np.zeros((L, 128, 2, 16), np.float32)
        for cc in range(2):
            sl = slice(cc * 128, (cc + 1) * 128)
            lruc[:, :, cc, 0:4] = inp['lru_conv_w'][:, :, sl].transpose(0, 2, 1)
            lruc[:, :, cc, 4] = inp['lru_conv_b'][:, sl]
            lruc[:, :, cc, 5] = inp['lru_br'][:, 0, sl]
            lruc[:, :, cc, 6] = inp['lru_br'][:, 1, sl]
            lruc[:, :, cc, 7] = inp['lru_bi'][:, 0, sl]
            lruc[:, :, cc, 8] = inp['lru_bi'][:, 1, sl]
            lruc[:, :, cc, 9] = inp['lru_lam'][:, 0, sl]
            lruc[:, :, cc, 10] = inp['lru_lam'][:, 1, sl]
            lruc[:, :, cc, 11] = inp['state_lru'][b, :, 0, sl]
            lruc[:, :, cc, 12] = inp['state_lru'][b, :, 1, sl]
        sd = inp['state_delta'][b]
        sd0 = np.zeros((L, 128, 4, 64), np.float32)
        for d in range(2):
            for h in range(4):
                pb = (h % 2) * 64
                sd0[:, pb:pb + 64, d * 2 + h // 2, :] = sd[:, d, h]
        maps.append({
            'xT': xT, 'wst': wsts, 'cst': carr, 'condT': condT, 'bmodT': bmodT, 'normwT': normwT, 'rope': rope,
            'qkw': qkw, 'onw': onw, 'dnvec': dnvec, 'dncw': dncw, 'lruc': lruc, 'lruw': lruw,
            'cachek': f(inp['cache_k'][b].reshape(L, PAST, 128)), 'cachev': f(inp['cache_v'][b].reshape(L, PAST, 128)),
            'sd0': f(sd0.reshape(L, 128, 256)),
        })
    return maps


def gather(res, cfg, nb):
    L, TS, TP = cfg['L'], cfg['TS'], cfg['TP']
    yp = np.zeros((2 * nb, TP, D), np.float32)
    ys = np.zeros((nb, TS, D), np.float32)
    nk = np.zeros((2 * nb, L, TP, 2, 64), np.float32)
    nv = np.zeros((2 * nb, L, TP, 2, 64), np.float32)
    nsd = np.zeros((2 * nb, L, 2, 4, 64, 64), np.float32)
    nsl = np.zeros((2 * nb, L, 2, 256), np.float32)
    for b in range(nb):
        r = res[b]
        y = r['yT'].transpose(1, 0, 2).reshape(D, -1).T
        ys[b] = y[:TS]
        yp[2 * b] = y[TS:TS + TP]
        yp[2 * b + 1] = y[TS + TP:]
        for p in range(2):
            nk[2 * b + p] = r['newk'][:, p * TP:(p + 1) * TP, :].reshape(L, TP, 2, 64)
            nv[2 * b + p] = r['newv'][:, p * TP:(p + 1) * TP, :].reshape(L, TP, 2, 64)
            sdo = r['newsd'][:, p].reshape(L, 128, 4, 64)
            for d in range(2):
                for h in range(4):
                    pb = (h % 2) * 64
                    nsd[2 * b + p, :, d, h] = sdo[:, pb:pb + 64, d * 2 + h // 2, :]
            slo = r['newsl'][:, p]
            for d in range(2):
                for cc in range(2):
                    nsl[2 * b + p, :, d, cc * 128:(cc + 1) * 128] = slo[:, :, d * 2 + cc]
    return yp, ys, nk, nv, nsd, nsl


def kernel(**inputs):
    inp = {k: np.asarray(v) for k, v in inputs.items()}
    nb, TS, _ = inp['x_sample'].shape
    TP = inp['x_prompt'].shape[1]
    L = inp['w_mod'].shape[0]
    PAST = inp['cache_k'].shape[2]
    cfg = dict(L=L, TS=TS, TP=TP, PAST=PAST)
    nc = build(cfg)
    maps = prep_inputs(inp, cfg)
    res = run_bass_kernel_spmd(nc, maps, core_ids=list(range(nb)))
    return gather(res.results, cfg, nb)
```

```python
import numpy as np
import concourse.bass as bass
import concourse.mybir as mybir
from concourse.bass_utils import run_bass_kernel_spmd

F32 = mybir.dt.float32
BF16 = mybir.dt.bfloat16
AF = mybir.ActivationFunctionType
ALU = mybir.AluOpType
AX = mybir.AxisListType

D = 1024
DFF = 2816
EPS = 1e-6
NIN = 5392
ENGS = ('pe', 'act', 'dve', 'pool', 'sp')


class Buf:
    def __init__(self, name, h):
        self.name = name
        self.h = h
        self.lw = None
        self.rd = {}
        self.sem = None
        self.dcnt = 0
        self.gen = 0
        self.psum = False

    def __getitem__(self, idx):
        return self.h[idx]


class Prog:
    def __init__(self, nc):
        self.nc = nc
        self.ops = {e: [] for e in ENGS}
        self.cnt = {e: 0 for e in ENGS}
        self.waited = {e: {} for e in ENGS}
        self.sems = []
        self.esem = {}
        for e in ENGS:
            self.esem[e] = self.newsem('e_' + e)
        self.dmabufs = []
        self.semcache = {}

    def newsem(self, name):
        s = self.nc.alloc_semaphore(name)
        self.sems.append(s)
        return len(self.sems) - 1

    def _need(self, eng, reads, writes, extra=()):
        need = {}

        def add(tok):
            if tok is None:
                return
            si, val, te = tok
            if te == 'pe' and eng == 'pe':
                return
            if need.get(si, 0) < val:
                need[si] = val
        for b in reads:
            add(b.lw)
            if b.psum:
                for si, (val, te) in b.rd.items():
                    if te != eng:
                        add((si, val, te))
        for b in writes:
            add(b.lw)
            for si, (val, te) in b.rd.items():
                add((si, val, te))
        for t in extra:
            add(t)
        w = self.waited[eng]
        out = []
        for si, val in need.items():
            if w.get(si, 0) < val:
                w[si] = val
                out.append((si, val))
        return out

    def _upd(self, tok, reads, writes):
        si, val, te = tok
        for b in reads:
            b.rd[si] = (val, te)
        for b in writes:
            b.lw = tok
            b.rd = {}

    def op(self, eng, fn, reads=(), writes=()):
        waits = self._need(eng, reads, writes)
        self.cnt[eng] += 1
        tok = (self.esem[eng], self.cnt[eng], eng)
        self._upd(tok, reads, writes)
        self.ops[eng].append((waits, fn, self.esem[eng], 1))

    def dma(self, q, out, in_, reads, writes, anchor, serialize=True, **kw):
        if anchor.sem is None:
            if anchor.name not in self.semcache:
                self.semcache[anchor.name] = [self.newsem('d_' + anchor.name), 0]
            anchor.sem = self.semcache[anchor.name][0]
        ent = self.semcache[anchor.name]
        extra = []
        if ent[1] and serialize:
            extra.append((anchor.sem, ent[1], 'dma'))
        waits = self._need(q, reads, writes, extra)
        ent[1] += 16
        anchor.dcnt = ent[1]
        tok = (anchor.sem, anchor.dcnt, 'dma')
        self._upd(tok, reads, writes)
        self.ops[q].append((waits, (lambda e: e.dma_start(out=out, in_=in_, **kw)), anchor.sem, 16))

    def barrier(self):
        toks = [(self.esem[e], self.cnt[e]) for e in ENGS if self.cnt[e]]
        toks += [(v[0], v[1]) for v in self.semcache.values()]
        for e in ENGS:
            w = self.waited[e]
            lst = []
            for si, val in toks:
                if si == self.esem[e]:
                    continue
                if w.get(si, 0) < val:
                    w[si] = val
                    lst.append((si, val))
            if lst:
                self.ops[e].append((lst, None, None, 0))

    def emit(self):
        nc = self.nc
        sems = self.sems
        ops = self.ops

        def run(e, lst):
            for waits, fn, si, inc in lst:
                for wsi, val in waits:
                    e.wait_ge(sems[wsi], val)
                if fn is not None:
                    fn(e).then_inc(sems[si], inc)
        with nc.Block() as block:
            @block.tensor
            def _(e):
                run(e, ops['pe'])

            @block.scalar
            def _(e):
                run(e, ops['act'])

            @block.vector
            def _(e):
                run(e, ops['dve'])

            @block.gpsimd
            def _(e):
                run(e, ops['pool'])

            @block.sync
            def _(e):
                run(e, ops['sp'])


class FreeList:
    def __init__(self, bufs):
        self.free = list(bufs)

    def get(self):
        return self.free.pop(0)

    def put(self, b):
        self.free.append(b)


def wstream_layout():
    lay = {}
    off = 0

    def add(name, n):
        nonlocal off
        lay[name] = (off, n)
        off += n
    for f in (1, 2):
        for b in range(11):
            add('gu%d_%d' % (f, b), 8 * 512)
        for mo in range(8):
            add('wd%d_%d' % (f, mo), 22 * 128)
    for i, n in enumerate((512, 512, 256)):
        add('infm_%d' % i, 8 * n)
    for i, n in enumerate((272, 512, 256)):
        add('intm_%d' % i, 8 * n)
    for b in range(3):
        for hf in range(2):
            add('gate_%d_%d' % (b, hf), 8 * 512)
    add('pa', 2 * 1024)
    add('pb', 4 * 1024)
    add('pc', 2 * 1024)
    for hf in range(2):
        add('wo_%d' % hf, 8 * 512)
    for b in range(18):
        add('mod_%d' % b, 8 * 512)
    return lay, off


def kc_layout(w, kc):
    return np.ascontiguousarray(w.reshape(kc, 128, w.shape[1]).transpose(1, 0, 2)).reshape(128, -1)


def build_wstream(inp, l):
    lay, tot = wstream_layout()
    ws = np.zeros((128, tot), np.float32)

    def put(name, arr):
        o, n = lay[name]
        assert arr.shape == (128, n), (name, arr.shape, n)
        ws[:, o:o + n] = arr
    ffw = {1: (inp['ffn1_wgu'], inp['ffn1_wd']), 2: (inp['ffn2_wgu'], inp['ffn2_wd'])}
    for f in (1, 2):
        wgu = ffw[f][0][l]
        g, u = wgu[:, :DFF], wgu[:, DFF:]
        for b in range(11):
            cols = np.concatenate([g[:, 256 * b:256 * b + 128], u[:, 256 * b:256 * b + 128],
                                   g[:, 256 * b + 128:256 * b + 256], u[:, 256 * b + 128:256 * b + 256]], axis=1)
            put('gu%d_%d' % (f, b), kc_layout(cols, 8))
        wd = ffw[f][1][l]
        for mo in range(8):
            put('wd%d_%d' % (f, mo), kc_layout(wd[:, mo * 128:(mo + 1) * 128], 22))
    w = inp['w_in'][l]
    dq, dk, dv, dz, db, da = w[:, 0:256], w[:, 256:512], w[:, 512:768], w[:, 768:1024], w[:, 1024:1032], w[:, 1032:1040]
    aq, ak, av, lx, ly = w[:, 1040:1552], w[:, 1552:1680], w[:, 1680:1808], w[:, 1808:2064], w[:, 2064:2320]
    gl = w[:, 2320:]
    aq = aq.reshape(D, 8, 64)[:, [0, 4, 1, 5, 2, 6, 3, 7], :].reshape(D, 512)
    fm = np.concatenate([dq, dk, dv, lx, ly], axis=1)
    for i, (a, n) in enumerate(((0, 512), (512, 512), (1024, 256))):
        put('infm_%d' % i, kc_layout(fm[:, a:a + n], 8))
    tm = np.concatenate([dz, db, da, aq, ak, av], axis=1)
    for i, (a, n) in enumerate(((0, 272), (272, 512), (784, 256))):
        put('intm_%d' % i, kc_layout(tm[:, a:a + n], 8))
    for b in range(3):
        for hf in range(2):
            put('gate_%d_%d' % (b, hf), kc_layout(gl[:, b * 1024 + hf * 512:b * 1024 + hf * 512 + 512], 8))
    put('pa', kc_layout(inp['w_pa'][l], 2))
    wpb = inp['w_pb'][l]
    rows = np.zeros((4, 128), np.int64)
    for g_ in range(4):
        rows[g_, :64] = g_ * 64 + np.arange(64)
        rows[g_, 64:] = (4 + g_) * 64 + np.arange(64)
    put('pb', kc_layout(wpb[rows.reshape(-1)], 4))
    put('pc', kc_layout(inp['w_pc'][l], 2))
    wo = inp['w_o'][l]
    for hf in range(2):
        put('wo_%d' % hf, kc_layout(wo[:, hf * 512:(hf + 1) * 512], 8))
    wm = inp['w_mod'][l]
    for b in range(18):
        put('mod_%d' % b, kc_layout(wm[:, b * 512:(b + 1) * 512], 8))
    return ws


def make_consts():
    i = np.arange(128)
    I, J = i[:, None], i[None, :]
    c = {}
    c['ident'] = (I == J)
    c['ones'] = np.ones((128, 128))
    c['UI'] = (I <= J)
    c['LI'] = (I >= J)
    c['blk64'] = ((I // 64) == (J // 64))
    bd = lambda n: ((I // n) == (J // n))
    c['bd16'] = bd(16)
    c['off32'] = bd(32) & ~bd(16)
    c['off64'] = bd(64) & ~bd(32)
    c['off128'] = ~bd(64)
    SL = (I > J)
    SU = (I < J)
    c['ms8'] = np.concatenate([np.tile(SL, (1, 4)), np.tile(SU, (1, 4))], axis=1)
    c['mst8'] = np.concatenate([np.tile(SU, (1, 4)), np.tile(SL, (1, 4))], axis=1)
    c['mit8'] = np.concatenate([np.tile(I <= J, (1, 4)), np.tile(I >= J, (1, 4))], axis=1)
    names = [n for n in c if n not in ('ms8', 'mst8', 'mit8')] + ['ms8', 'mst8', 'mit8']
    offs = {}
    o = 0
    for n in names:
        offs[n] = (o, c[n].shape[1])
        o += c[n].shape[1]
    arr = np.concatenate([c[n].astype(np.float32) for n in names], axis=1)
    return arr, offs


def build(cfg):
    L, TS, TP, PAST = cfg['L'], cfg['TS'], cfg['TP'], cfg['PAST']
    NT = TS + 2 * TP
    dbg = cfg.get('dbg', None)
    nc = bass.Bass("TRN2", target_bir_lowering=False)
    lay, WTOT = wstream_layout()
    carr, coffs = make_consts()
    NCST = carr.shape[1]

    def din(name, shape, dt=F32):
        return nc.dram_tensor(name, list(shape), dt, kind="ExternalInput").ap()

    def dout(name, shape, dt=F32):
        return nc.dram_tensor(name, list(shape), dt, kind="ExternalOutput").ap()

    def dscr(name, shape, dt=F32):
        return nc.dram_tensor(name, list(shape), dt, kind="Internal").ap()
    xT_in = din('xT', [128, 8, NT])
    wst = din('wst', [L, 128, WTOT])
    cst_in = din('cst', [128, NCST])
    condT_in = din('condT', [128, 8, 2])
    bmodT_in = din('bmodT', [L, 128, 72])
    normw_in = din('normwT', [L, 128, 24])
    rope_in = din('rope', [TS, 64])
    qkw_in = din('qkw', [L, 640])
    onw_in = din('onw', [L, 256])
    dnvec_in = din('dnvec', [L, 16])
    dncw_in = din('dncw', [L, 128, 24])
    lruc_in = din('lruc', [L, 128, 2, 16])
    lruw_in = din('lruw', [L, 128, 8, 128])
    ck_in = din('cachek', [L, PAST, 128])
    cv_in = din('cachev', [L, PAST, 128])
    sd_in = din('sd0', [L, 128, 256])
    yT = dout('yT', [128, 8, NT])
    ok_out = dout('newk', [L, 2 * TP, 128])
    ov_out = dout('newv', [L, 2 * TP, 128])
    osd_out = dout('newsd', [L, 2, 128, 256])
    osl_out = dout('newsl', [L, 2, 128, 4])
    fm_raw = dscr('fm_raw', [128, 10, NT])
    tm_raw = dscr('tm_raw', [NT, 1040])
    dnq = dscr('dnq', [128, 6, NT], BF16)
    obuf_d = dscr('obuf_d', [128, 8, NT], BF16)
    wbf = [dscr('wbf%d' % l, [128, WTOT], BF16) for l in range(L)]
    dbg_out = dout('dbg', [128, 8, NT]) if dbg else None

    P = Prog(nc)
    sb_off = [16384 + 256]
    sb_max = [0]

    def sb(name, shape, dt=F32):
        esz = 4 if dt == F32 else 2
        n = int(np.prod(shape[1:])) * esz
        n = (n + 31) // 32 * 32
        h = nc.alloc_sbuf_tensor_at(name, list(shape), dt, offset=sb_off[0])
        sb_off[0] += n
        sb_max[0] = max(sb_max[0], sb_off[0])
        assert sb_off[0] <= 222 * 1024, (name, sb_off[0])
        return Buf(name, h)
    psb = []
    for i in range(8):
        psb.append(Buf('ps%d' % i, nc.alloc_psum_tensor('ps%d' % i, [128, 512], F32)))
        psb[-1].psum = True
    PS = FreeList(psb)
    NCP = coffs['ms8'][0]
    cst = sb('cst', [128, NCP])
    cst_bf = sb('cst_bf', [128, 256], BF16)
    coef = sb('coef', [128, L, 3, 3, 8, 2])
    modT = sb('modT', [128, 72, 2])
    smallc = sb('smallc', [128, 64])
    ring = [sb('ring%d' % i, [128, 4096], BF16) for i in range(5)]
    ring_i = [0]
    PERM_END = sb_off[0]

    def C(name):
        o, n = coffs[name]
        return cst.h[:, o:o + n]
    ident_f = C('ident')
    ones_f = C('ones')
    ident_b = cst_bf.h[:, 0:128]
    ones_b = cst_bf.h[:, 128:256]

    def mm(out, lhsT, rhs, start, stop, R, W):
        P.op('pe', lambda e: e.matmul(out, lhsT, rhs, start=start, stop=stop), R, W)

    def tr(out, in_, R, W):
        P.op('pe', lambda e: e.transpose(out, in_, ident_f), R + [cst], W)

    def act(out, in_, func, R, W, bias=None, scale=1.0):
        if bias is None:
            P.op('act', lambda e: e.activation(out=out, in_=in_, func=func, scale=scale), R, W)
        else:
            P.op('act', lambda e: e.activation(out=out, in_=in_, func=func, bias=bias, scale=scale), list(R) + [smallc], W)

    def tt(eng, out, in0, in1, op, R, W):
        P.op(eng, lambda e: e.tensor_tensor(out=out, in0=in0, in1=in1, op=op), R, W)

    def ts(eng, out, in0, s1, s2, op0, op1, R, W):
        if s2 is None:
            P.op(eng, lambda e: e.tensor_scalar(out=out, in0=in0, scalar1=s1, scalar2=None, op0=op0), R, W)
        else:
            P.op(eng, lambda e: e.tensor_scalar(out=out, in0=in0, scalar1=s1, scalar2=s2, op0=op0, op1=op1), R, W)

    def stt(out, in0, scalar, in1, op0, op1, R, W):
        P.op('dve', lambda e: e.scalar_tensor_tensor(out=out, in0=in0, scalar=scalar, in1=in1, op0=op0, op1=op1), R, W)

    def cp(eng, out, in_, R, W):
        if eng == 'act':
            act(out, in_, AF.Copy, R, W)
        else:
            P.op(eng, lambda e: e.tensor_copy(out=out, in_=in_), R, W)

    def recip(out, in_, R, W):
        P.op('dve', lambda e: e.reciprocal(out=out, in_=in_), R, W)

    def memset(eng, out, val, W):
        P.op(eng, lambda e: e.memset(out, val), [], W)

    def ld(out, in_, W, anchor=None, R=()):
        P.dma('sp', out, in_, list(R), [W], anchor or W)

    def st(out, in_, Rb, Wd=(), anchor=None, q='sp'):
        P.dma(q, out, in_, [Rb], list(Wd), anchor or Rb)

    def wload(l, name):
        o, n = lay[name]
        slot = ring[ring_i[0] % 5]
        ring_i[0] += 1
        slot.gen += 1
        P.dma('sp', slot.h[:, 0:n], wbf[l][:, o:o + n], [wbfM if name.startswith('mod') else wbfL[l]], [slot], slot)
        return slot, n

    wbfL = [Buf('wbfL%d' % l, None) for l in range(L)]
    CW = 8192

    wbfM = Buf('wbfM', None)
    MOD0 = lay['mod_0'][0]

    def cast_weights(l):
        for c0 in range(0, MOD0, CW):
            c1 = min(MOD0, c0 + CW)
            P.dma('pool', wbf[l][:, c0:c1], wst[l, :, c0:c1], [], [wbfL[l]], wbfL[l], serialize=False, max_dma_last_dim=8192)
    for l in range(L):
        for c0 in range(MOD0, WTOT, CW):
            c1 = min(WTOT, c0 + CW)
            P.dma('pool', wbf[l][:, c0:c1], wst[l, :, c0:c1], [], [wbfM], wbfM, serialize=False, max_dma_last_dim=8192)
    cast_weights(0)

    ld(cst.h[:, :], cst_in[:, 0:NCP], cst)
    cp('dve', cst_bf.h[:, 0:128], ident_f, [cst], [cst_bf])
    cp('dve', cst_bf.h[:, 128:256], ones_f, [cst], [cst_bf])
    sb_off[0] = PERM_END
    condt = sb('condt', [128, 8, 2])
    condb = sb('condb', [128, 8, 2], BF16)
    sgc = sb('sgc', [128, 8, 2])
    bmt = sb('bmt', [128, 72])
    nwt = sb('nwt', [128, 24])
    ld(condt.h[:], condT_in[:], condt, anchor=cst)
    act(sgc.h[:], condt.h[:], AF.Sigmoid, [condt], [sgc])
    tt('dve', condb.h[:], condt.h[:], sgc.h[:], ALU.mult, [condt, sgc], [condb])
    for l in range(L):
        ld(bmt.h[:], bmodT_in[l], bmt, anchor=cst)
        ld(nwt.h[:], normw_in[l], nwt, anchor=cst)
        psM = PS.get()
        for blk in range(18):
            slot, n = wload(l, 'mod_%d' % blk)
            for jj in range(4):
                j = blk * 4 + jj
                for k in range(8):
                    mm(psM.h[:, 2 * j:2 * j + 2], slot.h[:, k * 512 + jj * 128:k * 512 + jj * 128 + 128],
                       condb.h[:, k, :], k == 0, k == 7, [slot, condb], [psM])
        tt('dve', modT.h[:], psM.h[:, 0:144].rearrange('p (j c) -> p j c', c=2),
           bmt.h[:].unsqueeze(2).broadcast_to([128, 72, 2]), ALU.add, [psM, bmt], [modT])
        PS.put(psM)
        mv = modT.h[:].rearrange('p (s m k) c -> p s m k c', s=3, m=3)
        ts('dve', coef.h[:, l, 0], mv[:, :, 1], 1.0, None, ALU.add, None, [modT], [coef])
        tt('dve', coef.h[:, l, 0], coef.h[:, l, 0],
           nwt.h[:].rearrange('p (s k) -> p s k', s=3).unsqueeze(3).broadcast_to([128, 3, 8, 2]), ALU.mult,
           [coef, nwt], [coef])
        cp('dve', coef.h[:, l, 1], mv[:, :, 0], [modT], [coef])
        ts('dve', coef.h[:, l, 2], mv[:, :, 2], 0.5, None, ALU.mult, None, [modT], [coef])
        ts('dve', coef.h[:, l, 2, 1], mv[:, 1, 2], 1.0, None, ALU.mult, None, [modT], [coef])
    P.barrier()

    sb_off[0] = PERM_END
    xsb = [sb('xs%d' % i, [128, 8, 512]) for i in range(2)]
    xs = xsb[0]
    cur = {'xs': xsb[0], 'pre': None, 'pend': None}
    xsst = [Buf('xsst0', None), Buf('xsst1', None)]
    sq = sb('sq', [128, 8, 512], BF16)
    rt = sb('rt', [128, 512])
    tmpA = [sb('tmpA%d' % i, [128, 512]) for i in range(2)]
    hb = sb('hb', [128, 8, 512], BF16)
    actb = sb('actb', [128, 22, 512], BF16)
    sgt = [sb('sgt%d' % i, [128, 512]) for i in range(2)]
    fmst = sb('fmst', [128, 10, 512])
    tmst = sb('tmst', [128, 4, 1040])
    merged = sb('merged', [128, 8, 512])
    mergedb = sb('mergedb', [128, 8, 512], BF16)
    gtb = [sb('gtb%d' % i, [128, 512]) for i in range(2)]
    obt = sb('obt', [128, 8, 512], BF16)
    AC_END = sb_off[0]
    cnt = [0]

    tiles = [(t0, 0) for t0 in range(0, TS, 512)] + [(TS, 1)]

    def sq_flush():
        mo = cur['pend']
        if mo is None:
            return
        mm(cur['pre'].h[:, :], ones_b, sq.h[:, mo, :], mo == 0, mo == 7, [cst_bf, sq], [cur['pre']])
        cur['pend'] = None

    def sq_accum(mo):
        xs = cur['xs']
        if mo == 0:
            cur['pre'] = PS.get()
        act(sq.h[:, mo, :], xs.h[:, mo, :], AF.Square, [xs], [sq])
        cur['pend'] = mo

    def norm_mod(l, s, c):
        xs = cur['xs']
        if cur['pre'] is not None:
            sq_flush()
            ps = cur['pre']
            cur['pre'] = None
        else:
            act(sq.h[:], xs.h[:], AF.Square, [xs], [sq])
            ps = PS.get()
            for k in range(8):
                mm(ps.h[:, :], ones_b, sq.h[:, k, :], k == 0, k == 7, [cst_bf, sq], [ps])
        act(rt.h[:], ps.h[:], AF.Sqrt, [ps], [rt], bias=smallc.h[:, 0:1], scale=1.0 / D)
        PS.put(ps)
        recip(rt.h[:], rt.h[:], [rt], [rt])
        for k in range(8):
            t = tmpA[k % 2]
            stt(t.h[:], xs.h[:, k, :], coef.h[:, l, 0, s, k, c:c + 1], rt.h[:], ALU.mult, ALU.mult, [xs, coef, rt], [t])
            act(hb.h[:, k, :], t.h[:], AF.Identity, [t, coef], [hb], bias=coef.h[:, l, 1, s, k, c:c + 1])

    def ffn(l, f, c, accum_next=True):
        s = 0 if f == 1 else 2
        xs = cur['xs']
        norm_mod(l, s, c)
        for b in range(11):
            slot, n = wload(l, 'gu%d_%d' % (f, b))
            for m2 in range(2):
                m = 2 * b + m2
                psg = PS.get()
                psu = PS.get()
                for k in range(8):
                    mm(psg.h[:, :], slot.h[:, k * 512 + m2 * 256:k * 512 + m2 * 256 + 128], hb.h[:, k, :], k == 0, k == 7, [slot, hb], [psg])
                for k in range(8):
                    mm(psu.h[:, :], slot.h[:, k * 512 + m2 * 256 + 128:k * 512 + m2 * 256 + 256], hb.h[:, k, :], k == 0, k == 7, [slot, hb], [psu])
                sg = sgt[m % 2]
                act(sg.h[:], psg.h[:], AF.Silu, [psg], [sg])
                PS.put(psg)
                tt('dve', actb.h[:, m, :], sg.h[:], psu.h[:], ALU.mult, [sg, psu], [actb])
                PS.put(psu)
        for mo in range(8):
            slot, n = wload(l, 'wd%d_%d' % (f, mo))
            ps = PS.get()
            for kc in range(22):
                mm(ps.h[:, :], slot.h[:, kc * 128:(kc + 1) * 128], actb.h[:, kc, :], kc == 0, kc == 21, [slot, actb], [ps])
            sq_flush()
            stt(xs.h[:, mo, :], ps.h[:], coef.h[:, l, 2, s, mo, c:c + 1], xs.h[:, mo, :], ALU.mult, ALU.add, [ps, coef, xs], [xs])
            PS.put(ps)
            if accum_next:
                sq_accum(mo)

    def win_proj(l, t0, c):
        norm_mod(l, 1, c)
        j = 0
        for i, ncol in enumerate((512, 512, 256)):
            slot, n = wload(l, 'infm_%d' % i)
            for jj in range(ncol // 128):
                ps = PS.get()
                for k in range(8):
                    mm(ps.h[:, :], slot.h[:, k * ncol + jj * 128:k * ncol + jj * 128 + 128], hb.h[:, k, :], k == 0, k == 7, [slot, hb], [ps])
                cp('act' if j % 2 else 'dve', fmst.h[:, j, :], ps.h[:], [ps], [fmst])
                PS.put(ps)
                j += 1
        st(fm_raw[:, :, t0:t0 + 512], fmst.h[:], fmst, q='pool')
        co = 0
        for i, ncol in enumerate((272, 512, 256)):
            slot, n = wload(l, 'intm_%d' % i)
            for tsub in range(4):
                ps = PS.get()
                for k in range(8):
                    mm(ps.h[:, 0:ncol], hb.h[:, k, tsub * 128:(tsub + 1) * 128], slot.h[:, k * ncol:(k + 1) * ncol], k == 0, k == 7, [slot, hb], [ps])
                cp('act' if tsub % 2 else 'dve', tmst.h[:, tsub, co:co + ncol], ps.h[:, 0:ncol], [ps], [tmst])
                PS.put(ps)
            co += ncol
        st(tm_raw[t0:t0 + 512, :].rearrange('(t p) c -> p t c', p=128), tmst.h[:], tmst, q='pool')

    def mix_out(l, t0, c):
        xs = cur['xs']
        norm_mod(l, 1, c)
        ld(obt.h[:], obuf_d[:, :, t0:t0 + 512], obt)
        chunks = {0: [0, 1], 1: [2, 3, 4, 5], 2: [6, 7]}
        pname = {0: 'pa', 1: 'pb', 2: 'pc'}
        for b in range(3):
            slotp, _ = wload(l, pname[b])
            nk = len(chunks[b])
            for hf in range(2):
                slotg, _ = wload(l, 'gate_%d_%d' % (b, hf))
                for jj in range(4):
                    j = hf * 4 + jj
                    ps1 = PS.get()
                    for k in range(8):
                        mm(ps1.h[:, :], slotg.h[:, k * 512 + jj * 128:k * 512 + jj * 128 + 128], hb.h[:, k, :], k == 0, k == 7, [slotg, hb], [ps1])
                    g = gtb[j % 2]
                    act(g.h[:], ps1.h[:], AF.Sigmoid, [ps1], [g])
                    PS.put(ps1)
                    ps2 = PS.get()
                    for kc in range(nk):
                        mm(ps2.h[:, :], slotp.h[:, kc * 1024 + j * 128:kc * 1024 + j * 128 + 128],
                           obt.h[:, chunks[b][kc], :], kc == 0, kc == nk - 1, [slotp, obt], [ps2])
                    if b == 0:
                        tt('dve', merged.h[:, j, :], g.h[:], ps2.h[:], ALU.mult, [g, ps2], [merged])
                    else:
                        tt('dve', g.h[:], g.h[:], ps2.h[:], ALU.mult, [g, ps2], [g])
                        if b == 1:
                            tt('pool', merged.h[:, j, :], merged.h[:, j, :], g.h[:], ALU.add, [merged, g], [merged])
                        else:
                            tt('pool', mergedb.h[:, j, :], merged.h[:, j, :], g.h[:], ALU.add, [merged, g], [mergedb])
                    PS.put(ps2)
        for hf in range(2):
            slot, _ = wload(l, 'wo_%d' % hf)
            for jj in range(4):
                j = hf * 4 + jj
                ps = PS.get()
                for k in range(8):
                    mm(ps.h[:, :], slot.h[:, k * 512 + jj * 128:k * 512 + jj * 128 + 128], mergedb.h[:, k, :], k == 0, k == 7, [slot, mergedb], [ps])
                sq_flush()
                stt(xs.h[:, j, :], ps.h[:], coef.h[:, l, 2, 1, j, c:c + 1], xs.h[:, j, :], ALU.mult, ALU.add, [ps, coef, xs], [xs])
                PS.put(ps)
                sq_accum(j)

    memset('dve', smallc.h[:, 0:1], EPS, [smallc])
    memset('dve', smallc.h[:, 1:2], 1.0, [smallc])

    def phaseAC(l_c, l_a):
        for ti, (t0, c) in enumerate(tiles):
            src = xT_in if l_c is None else yT
            xs = xsb[ti % 2]
            cur['xs'] = xs
            cur['pre'] = None
            cur['pend'] = None
            if ti == 0:
                ld(xs.h[:], src[:, :, t0:t0 + 512], xs)
            if ti + 1 < len(tiles):
                tn0 = tiles[ti + 1][0]
                ld(xsb[(ti + 1) % 2].h[:], src[:, :, tn0:tn0 + 512], xsb[(ti + 1) % 2])
            if l_c is not None:
                mix_out(l_c, t0, c)
                ffn(l_c, 2, c, accum_next=(l_a is not None))
            if l_a is not None:
                ffn(l_a, 1, c, accum_next=True)
                win_proj(l_a, t0, c)
            st(yT[:, :, t0:t0 + 512], xs.h[:], xs, q='pool', anchor=xsst[ti % 2])
        P.barrier()

    seqs = [(0, TS, 0), (TS, TP, 1), (TS + TP, TP, 2)]

    def phaseB(l):
        sb_off[0] = PERM_END
        lru(l)
        P.barrier()
        sb_off[0] = PERM_END
        attention(l)
        P.barrier()
        sb_off[0] = PERM_END
        deltanet(l)
        P.barrier()

    def lru(l):
        TM = max(TS, TP)
        lc = sb('lc', [128, 2, 16])
        lw = sb('lw', [128, 8, 128])
        sp = sb('sp', [128, 2, 4])
        B0 = sb('B0', [128, TM + 3])
        B1 = sb('B1', [128, TM])
        B2 = sb('B2', [128, TM])
        B3 = sb('B3', [128, TM])
        B4 = sb('B4', [128, TM])
        B5 = sb('B5', [128, TM])
        Bo = sb('Bo', [128, TM], BF16)
        slst = sb('slst', [128, 2, 4])
        ld(lc.h[:], lruc_in[l], lc)
        ld(lw.h[:], lruw_in[l], lw, anchor=lc)
        for cc in range(2):
            act(sp.h[:, cc, 0:2], lc.h[:, cc, 9:11], AF.Exp, [lc], [sp], scale=-1.0)
            act(sp.h[:, cc, 0:2], sp.h[:, cc, 0:2], AF.Ln, [sp], [sp], bias=smallc.h[:, 1:2])
            ts('dve', sp.h[:, cc, 2:4], sp.h[:, cc, 0:2], -16.0, None, ALU.mult, None, [sp], [sp])
            ts('dve', sp.h[:, cc, 0:2], sp.h[:, cc, 0:2], -8.0, None, ALU.mult, None, [sp], [sp])
        for (s0, T, si) in seqs:
            for cc in range(2):
                memset('dve', B0.h[:, 0:2], 0.0, [B0])
                memset('dve', B0.h[:, T + 2:T + 3], 0.0, [B0])
                ld(B0.h[:, 2:T + 2], fm_raw[:, 6 + cc, s0:s0 + T], B0)
                ts('dve', B1.h[:, 0:T], B0.h[:, 0:T], lc.h[:, cc, 0:1], lc.h[:, cc, 4:5], ALU.mult, ALU.add, [B0, lc], [B1])
                for j in range(1, 4):
                    stt(B1.h[:, 0:T], B0.h[:, j:j + T], lc.h[:, cc, j:j + 1], B1.h[:, 0:T], ALU.mult, ALU.add, [B0, lc, B1], [B1])
                for d in range(2):
                    for gi, dst in ((0, B2), (1, B4)):
                        for ct in range(0, T, 512):
                            w_ = min(512, T - ct)
                            ps = PS.get()
                            mm(ps.h[:, 0:w_], lw.h[:, gi * 4 + d * 2 + cc, :], B1.h[:, ct:ct + w_], True, True, [lw, B1], [ps])
                            act(dst.h[:, ct:ct + w_], ps.h[:, 0:w_], AF.Sigmoid, [ps, lc], [dst], bias=lc.h[:, cc, 5 + gi * 2 + d:6 + gi * 2 + d])
                            PS.put(ps)
                    act(B3.h[:, 0:T], B2.h[:, 0:T], AF.Exp, [B2, sp], [B3], scale=sp.h[:, cc, 2 + d:3 + d])
                    act(B2.h[:, 0:T], B2.h[:, 0:T], AF.Exp, [B2, sp], [B2], scale=sp.h[:, cc, d:d + 1])
                    ts('dve', B3.h[:, 0:T], B3.h[:, 0:T], -1.0, 1.0, ALU.mult, ALU.add, [B3], [B3])
                    act(B3.h[:, 0:T], B3.h[:, 0:T], AF.Sqrt, [B3], [B3])
                    tt('dve', B3.h[:, 0:T], B3.h[:, 0:T], B4.h[:, 0:T], ALU.mult, [B3, B4], [B3])
                    tt('dve', B3.h[:, 0:T], B3.h[:, 0:T], B1.h[:, 0:T], ALU.mult, [B3, B1], [B3])
                    h0 = lc.h[:, cc, 11 + d:12 + d] if si == 0 else 0.0
                    dst = B5 if d == 0 else B4
                    if d == 0:
                        P.op('dve', (lambda e, o=dst.h[:, 0:T], a=B2.h[:, 0:T], u=B3.h[:, 0:T], h0=h0:
                                     e.tensor_tensor_scan(out=o, data0=a, data1=u, initial=h0, op0=ALU.mult, op1=ALU.add)),
                             [B2, B3, lc], [dst])
                    else:
                        P.op('dve', (lambda e, o=dst.h[:, 0:T][:, ::-1], a=B2.h[:, 0:T][:, ::-1], u=B3.h[:, 0:T][:, ::-1], h0=h0:
                                     e.tensor_tensor_scan(out=o, data0=a, data1=u, initial=h0, op0=ALU.mult, op1=ALU.add)),
                             [B2, B3, lc], [dst])
                if si > 0:
                    cp('dve', slst.h[:, si - 1, cc:cc + 1], B5.h[:, T - 1:T], [B5], [slst])
                    cp('dve', slst.h[:, si - 1, 2 + cc:3 + cc], B4.h[:, 0:1], [B4], [slst])
                    if cc == 1:
                        st(osl_out[l, si - 1], slst.h[:, si - 1, :], slst, anchor=Bo)
                tt('dve', B5.h[:, 0:T], B5.h[:, 0:T], B4.h[:, 0:T], ALU.add, [B5, B4], [B5])
                ld(B0.h[:, 0:T], fm_raw[:, 8 + cc, s0:s0 + T], B0)
                tt('dve', B2.h[:, 0:T], B0.h[:, 0:T], B0.h[:, 0:T], ALU.mult, [B0], [B2])
                ts('dve', B2.h[:, 0:T], B2.h[:, 0:T], 0.044715, 1.0, ALU.mult, ALU.add, [B2], [B2])
                tt('dve', B2.h[:, 0:T], B2.h[:, 0:T], B0.h[:, 0:T], ALU.mult, [B2, B0], [B2])
                act(B2.h[:, 0:T], B2.h[:, 0:T], AF.Sigmoid, [B2], [B2], scale=1.5957691216057308)
                tt('dve', B2.h[:, 0:T], B2.h[:, 0:T], B0.h[:, 0:T], ALU.mult, [B2, B0], [B2])
                tt('dve', Bo.h[:, 0:T], B2.h[:, 0:T], B5.h[:, 0:T], ALU.mult, [B2, B5], [Bo])
                st(obuf_d[:, 6 + cc, s0:s0 + T], Bo.h[:, 0:T], Bo)

    def attention(l):
        NKS = PAST + TS
        kT = sb('kT', [128, NKS], BF16)
        NCK = NKS // 128
        vaug = sb('vaug', [128, NCK, 2, 128], BF16)
        qT = sb('qT', [128, 2, 4, TS], BF16)
        wqk = sb('wqk', [128, 640])
        rin = [sb('rin%d' % i, [128, 768]) for i in range(2)]
        rp = [sb('rp%d' % i, [128, 64]) for i in range(2)]
        sqq = sb('sqq', [128, 640])
        ssq = sb('ssq', [128, 10])
        qn = sb('qn', [128, 640])
        qr = sb('qr', [128, 640])
        t1 = sb('t1', [128, 320])
        t2 = sb('t2', [128, 320])
        pT = [sb('pT%d' % i, [128, 512], BF16) for i in range(5)]
        rr = sb('rr', [128, 512])
        bc = sb('bc', [128, 512])
        obst = [sb('obst%d' % i, [128, 4, 128], BF16) for i in range(2)]
        ld(wqk.h[:], qkw_in[l:l + 1, :].broadcast_to([128, 640]), wqk, anchor=rin[0])
        memset('pool', qT.h[64:128, 0], 0.0, [qT])
        memset('pool', qT.h[0:64, 1], 0.0, [qT])
        memset('pool', vaug.h[:, :, 0, 64:128], 1.0, [vaug])
        memset('pool', vaug.h[:, :, 1, 0:64], 1.0, [vaug])
        cntl = [0]
        for (s0, T, si) in seqs:
            koff = PAST if si == 0 else 0
            nq = T // 128
            if si == 0:
                for ck in range(PAST // 128):
                    r = rin[cntl[0] % 2]
                    cntl[0] += 1
                    ld(r.h[:, 0:128], ck_in[l, ck * 128:(ck + 1) * 128, :], r)
                    ld(r.h[:, 128:256], cv_in[l, ck * 128:(ck + 1) * 128, :], r)
                    ps = PS.get()
                    tr(ps.h[:, 0:128], r.h[:, 0:128], [r], [ps])
                    cp('act', kT.h[:, ck * 128:(ck + 1) * 128], ps.h[:, 0:128], [ps], [kT])
                    PS.put(ps)
                    cp('dve', vaug.h[:, ck, 0, 0:64], r.h[:, 128:192], [r], [vaug])
                    cp('dve', vaug.h[:, ck, 1, 64:128], r.h[:, 192:256], [r], [vaug])
            for cq in range(nq):
                r = rin[cntl[0] % 2]
                rpp = rp[cntl[0] % 2]
                cntl[0] += 1
                tok = s0 + cq * 128
                ld(r.h[:], tm_raw[tok:tok + 128, 272:1040], r)
                if si == 0:
                    ld(rpp.h[:], rope_in[cq * 128:(cq + 1) * 128, :], rpp, anchor=r)
                act(sqq.h[:], r.h[:, 0:640], AF.Square, [r], [sqq])
                P.op('dve', lambda e, o=ssq.h[:], i=sqq.h[:].rearrange('p (h d) -> p h d', d=64): e.tensor_reduce(out=o, in_=i, axis=AX.X, op=ALU.add), [sqq], [ssq])
                act(ssq.h[:], ssq.h[:], AF.Sqrt, [ssq], [ssq], bias=smallc.h[:, 0:1], scale=1.0 / 64)
                recip(ssq.h[:], ssq.h[:], [ssq], [ssq])
                tt('dve', qn.h[:].rearrange('p (h d) -> p h d', d=64), r.h[:, 0:640].rearrange('p (h d) -> p h d', d=64),
                   ssq.h[:].unsqueeze(2).broadcast_to([128, 10, 64]), ALU.mult, [r, ssq], [qn])
                if si == 0:
                    tt('dve', qn.h[:], qn.h[:], wqk.h[:], ALU.mult, [qn, wqk], [qn])
                    qv = qn.h[:].rearrange('p (h a x r) -> p h a x r', h=10, a=2, x=2)
                    ov = qr.h[:].rearrange('p (h a x r) -> p h a x r', h=10, a=2, x=2)
                    cs = rpp.h[:, 0:32].rearrange('p (a r) -> p a r', a=2).unsqueeze(1).broadcast_to([128, 10, 2, 16])
                    sn = rpp.h[:, 32:64].rearrange('p (a r) -> p a r', a=2).unsqueeze(1).broadcast_to([128, 10, 2, 16])
                    t1v = t1.h[:].rearrange('p (h a r) -> p h a r', h=10, a=2)
                    t2v = t2.h[:].rearrange('p (h a r) -> p h a r', h=10, a=2)
                    tt('dve', t1v, qv[:, :, :, 0, :], cs, ALU.mult, [qn, rpp], [t1])
                    tt('pool', t2v, qv[:, :, :, 1, :], sn, ALU.mult, [qn, rpp], [t2])
                    tt('dve', ov[:, :, :, 0, :], t1v, t2v, ALU.subtract, [t1, t2], [qr])
                    tt('dve', t1v, qv[:, :, :, 1, :], cs, ALU.mult, [qn, rpp], [t1])
                    tt('pool', t2v, qv[:, :, :, 0, :], sn, ALU.mult, [qn, rpp], [t2])
                    tt('dve', ov[:, :, :, 1, :], t1v, t2v, ALU.add, [t1, t2], [qr])
                else:
                    tt('dve', qr.h[:], qn.h[:], wqk.h[:], ALU.mult, [qn, wqk], [qr])
                    pt = (si - 1) * TP + cq * 128
                    st(ok_out[l, pt:pt + 128, :], qr.h[:, 512:640], qr)
                    st(ov_out[l, pt:pt + 128, :], r.h[:, 640:768], r)
                qrv = qr.h[:].rearrange('p (h d) -> p h d', d=64)
                for g in range(4):
                    ps = PS.get()
                    tr(ps.h[:, 0:128], qr.h[:, g * 128:(g + 1) * 128], [qr], [ps])
                    cp('act' if g % 2 else 'dve', qT.h[0:64, 0, g, cq * 128:(cq + 1) * 128], ps.h[0:64, 0:128], [ps], [qT])
                    cp('act' if g % 2 else 'dve', qT.h[64:128, 1, g, cq * 128:(cq + 1) * 128], ps.h[64:128, 0:128], [ps], [qT])
                    PS.put(ps)
                ps = PS.get()
                tr(ps.h[:, 0:128], qr.h[:, 512:640], [qr], [ps])
                kc0 = koff + cq * 128
                cp('act', kT.h[:, kc0:kc0 + 128], ps.h[:, 0:128], [ps], [kT])
                PS.put(ps)
                ckk = kc0 // 128
                cp('dve', vaug.h[:, ckk, 0, 0:64], r.h[:, 640:704], [r], [vaug])
                cp('pool', vaug.h[:, ckk, 1, 64:128], r.h[:, 704:768], [r], [vaug])
            nck = (koff + T) // 128
            pi = 0
            for qb in range(nq):
                for kv in range(2):
                    psO = PS.get()
                    pend = []

                    def pv(ck, p):
                        mm(psO.h[:, :], vaug.h[:, ck, kv, :], p.h[:], ck == 0, ck == nck - 1, [vaug, p], [psO])
                    for ck in range(nck):
                        psS = PS.get()
                        mm(psS.h[:, :], kT.h[:, ck * 128:(ck + 1) * 128],
                           qT.h[:, kv, :, qb * 128:(qb + 1) * 128], True, True, [kT, qT], [psS])
                        p = pT[pi % 5]
                        pi += 1
                        act(p.h[:], psS.h[:], AF.Exp, [psS], [p], scale=0.125)
                        PS.put(psS)
                        pend.append((ck, p))
                        if len(pend) > 2:
                            pv(*pend.pop(0))
                    while pend:
                        pv(*pend.pop(0))
                    p0 = 64 if kv == 0 else 0
                    o0 = 0 if kv == 0 else 64
                    recip(rr.h[p0:p0 + 1, :], psO.h[p0:p0 + 1, :], [psO], [rr])
                    psB = PS.get()
                    mm(psB.h[:, :], ones_f[p0:p0 + 1, :], rr.h[p0:p0 + 1, :], True, True, [cst, rr], [psB])
                    cp('act', bc.h[o0:o0 + 64, :], psB.h[o0:o0 + 64, :], [psB], [bc])
                    PS.put(psB)
                    ob = obst[qb % 2]
                    tt('dve', ob.h[o0:o0 + 64, :, :],
                       psO.h[o0:o0 + 64, :].rearrange('p (g t) -> p g t', g=4),
                       bc.h[o0:o0 + 64, :].rearrange('p (g t) -> p g t', g=4), ALU.mult, [psO, bc], [ob])
                    PS.put(psO)
                st(obuf_d[:, 2:6, s0 + qb * 128:s0 + (qb + 1) * 128], obst[qb % 2].h[:], obst[qb % 2])

    def deltanet(l):
        TM = max(TS, TP)
        cw = sb('cw', [128, 24])
        dv = sb('dv', [128, 16])
        nea = sb('nea', [128, 8])
        onw = sb('onw', [128, 256])
        R0 = sb('R0', [128, TM + 3])
        R1 = sb('R1', [128, TM])
        R2 = sb('R2', [128, TM])
        R1b = sb('R1b', [128, TM], BF16)
        ld(cw.h[:], dncw_in[l], cw)
        ld(dv.h[:], dnvec_in[l:l + 1, :].broadcast_to([128, 16]), dv, anchor=cw)
        ld(onw.h[:], onw_in[l:l + 1, :].broadcast_to([128, 256]), onw, anchor=cw)
        act(nea.h[:], dv.h[:, 0:8], AF.Exp, [dv], [nea])
        ts('dve', nea.h[:], nea.h[:], -1.0, None, ALU.mult, None, [nea], [nea])
        blk64 = C('blk64')
        for (s0, T, si) in seqs:
            for cq in range(6):
                memset('dve', R0.h[:, 0:2], 0.0, [R0])
                memset('dve', R0.h[:, T + 2:T + 3], 0.0, [R0])
                ld(R0.h[:, 2:T + 2], fm_raw[:, cq, s0:s0 + T], R0)
                ts('dve', R1.h[:, 0:T], R0.h[:, 0:T], cw.h[:, cq * 4:cq * 4 + 1], None, ALU.mult, None, [R0, cw], [R1])
                for j in range(1, 4):
                    stt(R1.h[:, 0:T], R0.h[:, j:j + T], cw.h[:, cq * 4 + j:cq * 4 + j + 1], R1.h[:, 0:T], ALU.mult, ALU.add, [R0, cw, R1], [R1])
                if cq >= 4:
                    act(R1b.h[:, 0:T], R1.h[:, 0:T], AF.Silu, [R1], [R1b])
                else:
                    act(R1.h[:, 0:T], R1.h[:, 0:T], AF.Silu, [R1], [R1])
                if cq < 4:
                    tt('pool', R2.h[:, 0:T], R1.h[:, 0:T], R1.h[:, 0:T], ALU.mult, [R1], [R2])
                    for ct in range(0, T, 512):
                        w_ = min(512, T - ct)
                        ps = PS.get()
                        mm(ps.h[:, 0:w_], blk64, R2.h[:, ct:ct + w_], True, True, [cst, R2], [ps])
                        act(R0.h[:, ct:ct + w_], ps.h[:, 0:w_], AF.Sqrt, [ps], [R0], bias=smallc.h[:, 0:1])
                        PS.put(ps)
                    recip(R0.h[:, 0:T], R0.h[:, 0:T], [R0], [R0])
                    if cq < 2:
                        stt(R1b.h[:, 0:T], R1.h[:, 0:T], 0.125, R0.h[:, 0:T], ALU.mult, ALU.mult, [R1, R0], [R1b])
                    else:
                        tt('dve', R1b.h[:, 0:T], R1.h[:, 0:T], R0.h[:, 0:T], ALU.mult, [R1, R0], [R1b])
                st(dnq[:, cq, s0:s0 + T], R1b.h[:, 0:T], R1b, anchor=R1b)
        P.barrier()
        import os
        DNS = int(os.environ.get('DNSTOP', '100'))
        if DNS == 0:
            return
        sb_off[0] = PERM_END
        cw = None
        S = sb('S', [128, 4, 64])
        dv2 = sb('dv2', [128, 16])
        nea = sb('nea2', [128, 8])
        onw = sb('onw2', [128, 256])
        ld(dv2.h[:], dnvec_in[l:l + 1, :].broadcast_to([128, 16]), dv2, anchor=S)
        ld(onw.h[:], onw_in[l:l + 1, :].broadcast_to([128, 256]), onw, anchor=S)
        act(nea.h[:], dv2.h[:, 0:8], AF.Exp, [dv2], [nea])
        ts('dve', nea.h[:], nea.h[:], -1.0, None, ALU.mult, None, [nea], [nea])
        NCH = TM // 128
        oacc = sb('oacc', [128, NCH, 256])
        qkfm = [[sb('qkfm%d_%d' % (i, d), [128, 4, 128], BF16) for d in range(2)] for i in range(2)]
        vfm = [[sb('vfm%d_%d' % (i, d), [128, 2, 128], BF16) for d in range(2)] for i in range(2)]
        ba = [[sb('ba%d_%d' % (i, d), [128, 16]) for d in range(2)] for i in range(2)]
        ktm = sb('ktm', [128, 8, 64])
        vtm = sb('vtm', [128, 8, 64])
        gcb = sb('gcb', [128, 16])
        gg = sb('gg', [128, 8])
        gct = sb('gct', [128, 16])
        sm = sb('sm', [128, 8, 8])
        gt2 = sb('gt2', [128, 4])
        dg = sb('dg', [128, 16, 128])
        big = FreeList([sb('big%d' % i, [128, 8, 128]) for i in range(7)])
        bigB = FreeList([sb('bigB%d' % i, [128, 8, 128], BF16) for i in range(10)])
        Sb = sb('Sb', [128, 4, 64], BF16)
        bv = sb('bv', [128, 8, 64], BF16)
        usb = sb('usb', [128, 8, 64])
        wsb = sb('wsb', [128, 8, 64], BF16)
        wkT = sb('wkT', [128, 8, 128], BF16)
        qdT = sb('qdT', [128, 8, 128], BF16)
        kz = sb('kz', [128, 8, 128], BF16)
        Bkz = sb('Bkz', [128, 8, 128], BF16)
        kdz = sb('kdz', [128, 8, 128], BF16)
        memset('pool', qdT.h[:], 0.0, [qdT])
        memset('pool', kz.h[:], 0.0, [kz])
        memset('pool', Bkz.h[:], 0.0, [Bkz])
        memset('pool', kdz.h[:], 0.0, [kdz])
        zt = sb('zt', [128, 256])
        od = sb('od', [128, 256])
        osq = sb('osq', [128, 256])
        oss = sb('oss', [128, 4])
        mk = sb('mk', [128, 3, 8, 128])
        ld(mk.h[:], cst_in[:, NCP:NCP + 3072].rearrange('p (m u c) -> p m u c', m=3, u=8), mk, anchor=S)
        ms8 = mk.h[:, 0]
        mst8 = mk.h[:, 1]
        mit8 = mk.h[:, 2]
        odst = [sb('odst%d' % i, [128, 2, 128], BF16) for i in range(2)]
        NCA = NT // 128
        BA = sb('BA', [128, NCA, 16])
        beta_all = sb('beta_all', [128, NCA, 8])
        g_all = sb('g_all', [128, NCA, 8])
        for c0 in range(0, NCA, 8):
            c1 = min(NCA, c0 + 8)
            ld(BA.h[:, c0:c1, :], tm_raw[c0 * 128:c1 * 128, 256:272].rearrange('(c p) k -> p c k', p=128), BA, anchor=S)
        act(beta_all.h[:], BA.h[:, :, 0:8], AF.Exp, [BA], [beta_all], scale=-1.0)
        ts('dve', beta_all.h[:], beta_all.h[:], 1.0, None, ALU.add, None, [beta_all], [beta_all])
        recip(beta_all.h[:], beta_all.h[:], [beta_all], [beta_all])
        tt('dve', g_all.h[:], BA.h[:, :, 8:16], dv2.h[:, 8:16].unsqueeze(1).broadcast_to([128, NCA, 8]), ALU.add, [BA, dv2], [g_all])
        act(g_all.h[:], g_all.h[:], AF.Exp, [g_all], [g_all])
        act(g_all.h[:], g_all.h[:], AF.Ln, [g_all], [g_all], bias=smallc.h[:, 1:2])
        tt('dve', g_all.h[:], g_all.h[:], nea.h[:].unsqueeze(1).broadcast_to([128, NCA, 8]), ALU.mult, [g_all, nea], [g_all])
        allc = sb('allc', [128, NCA, 6, 8])
        n8 = NCA * 8
        g2d = g_all.h[:].rearrange('p c k -> p (c k)')
        psA = PS.get()
        psB = PS.get()
        mm(psA.h[:, 0:n8], C('UI'), g2d, True, True, [cst, g_all], [psA])
        mm(psB.h[:, 0:n8], C('LI'), g2d, True, True, [cst, g_all], [psB])
        cp('dve', allc.h[:, :, 0, 0:4], psA.h[:, 0:n8].rearrange('p (c k) -> p c k', k=8)[:, :, 0:4], [psA], [allc])
        cp('dve', allc.h[:, :, 0, 4:8], psB.h[:, 0:n8].rearrange('p (c k) -> p c k', k=8)[:, :, 4:8], [psB], [allc])
        PS.put(psA)
        PS.put(psB)
        psC = PS.get()
        mm(psC.h[:, 0:n8], ones_f, g2d, True, True, [cst, g_all], [psC])
        totv = psC.h[:, 0:n8].rearrange('p (c k) -> p c k', k=8)
        tt('dve', allc.h[:, :, 3, :], totv, allc.h[:, :, 0, :], ALU.subtract, [psC, allc], [allc])
        act(allc.h[:, :, 4, :], totv, AF.Exp, [psC], [allc])
        PS.put(psC)
        act(allc.h[:, :, 3, :], allc.h[:, :, 3, :], AF.Exp, [allc], [allc])
        act(allc.h[:, :, 2, :], allc.h[:, :, 0, :], AF.Exp, [allc], [allc])
        cp('dve', allc.h[:, :, 1, :], beta_all.h[:], [beta_all], [allc])
        tt('dve', allc.h[:, :, 5, :], allc.h[:, :, 2, :], beta_all.h[:], ALU.mult, [allc, beta_all], [allc])

        def bc8(name):
            return C(name).unsqueeze(1).broadcast_to([128, 8, 128])
        stepi = [0]

        def stage(lhsT_t, rhs_t, evac, ident=False):
            for gi in range(2):
                ps = PS.get()
                for uu in range(4):
                    u = gi * 4 + uu
                    if ident:
                        mm(ps.h[:, uu * 128:(uu + 1) * 128], lhsT_t.h[:, u, :], ident_b, True, True, [lhsT_t, cst_bf], [ps])
                    else:
                        mm(ps.h[:, uu * 128:(uu + 1) * 128], lhsT_t.h[:, u, :], rhs_t.h[:, u, :], True, True, [lhsT_t, rhs_t], [ps])
                evac(gi, ps)
                PS.put(ps)

        def g4(t, gi):
            return t.h[:, gi * 4:(gi + 1) * 4, :]

        def p4(ps):
            return ps.h[:, :].rearrange('p (u c) -> p u c', u=4)

        def ev_copy(dst, eng='act'):
            def f(gi, ps):
                cp(eng, g4(dst, gi), p4(ps), [ps], [dst])
            return f

        def ev_mask(dst, mname):
            mk4 = C(mname).unsqueeze(1).broadcast_to([128, 4, 128])

            def f(gi, ps):
                tt('dve', g4(dst, gi), p4(ps), mk4, ALU.mult, [ps, cst], [dst])
            return f

        def ev_add(dst, base, op):
            def f(gi, ps):
                tt('dve', g4(dst, gi), g4(base, gi), p4(ps), op, [ps, base], [dst])
            return f

        if DNS == 10:
            return
        for (s0, T, si) in seqs:
            N = T // 128
            if si == 0:
                ld(S.h[:], sd_in[l].rearrange('p (i v) -> p i v', i=4), S)
            else:
                memset('dve', S.h[:], 0.0, [S])
            cp('dve', Sb.h[:], S.h[:], [S], [Sb])
            seen = set()
            for n in range(N):
                par = stepi[0] % 2
                stepi[0] += 1
                cis = (n, N - 1 - n)
                for d in range(2):
                    tok = s0 + cis[d] * 128
                    ld(qkfm[par][d].h[:], dnq[:, 0:4, tok:tok + 128], qkfm[par][d], anchor=qkfm[par][0])
                    ld(vfm[par][d].h[:], dnq[:, 4:6, tok:tok + 128], vfm[par][d], anchor=qkfm[par][0])
                for d in range(2):
                    for c2 in range(2):
                        ps = PS.get()
                        mm(ps.h[:, 0:128], qkfm[par][d].h[:, 2 + c2, :], ident_b, True, True, [qkfm[par][d], cst_bf], [ps])
                        cp('act', ktm.h[:, d * 4 + c2 * 2:d * 4 + c2 * 2 + 2, :], ps.h[:, 0:128].rearrange('p (h c) -> p h c', h=2), [ps], [ktm])
                        PS.put(ps)
                        ps = PS.get()
                        mm(ps.h[:, 0:128], vfm[par][d].h[:, c2, :], ident_b, True, True, [vfm[par][d], cst_bf], [ps])
                        cp('dve', vtm.h[:, d * 4 + c2 * 2:d * 4 + c2 * 2 + 2, :], ps.h[:, 0:128].rearrange('p (h c) -> p h c', h=2), [ps], [vtm])
                        PS.put(ps)
                if DNS == 11:
                    return
                gci = [(s0 + cis[d] * 128) // 128 for d in range(2)]
                for d in range(2):
                    cp('pool', gcb.h[:].rearrange('p (r k) -> p r k', r=2)[:, :, d * 4:(d + 1) * 4],
                       allc.h[:, gci[d], 0:2, d * 4:(d + 1) * 4], [allc], [gcb])
                    cp('act', sm.h[:, 0:4, d * 4:(d + 1) * 4], allc.h[:, gci[d], 2:6, d * 4:(d + 1) * 4], [allc], [sm])
                if DNS == 12:
                    return
                cp('dve', gt2.h[0:64, :], sm.h[0:64, 2, 0:8:2], [sm], [gt2])
                cp('dve', gt2.h[64:128, :], sm.h[64:128, 2, 1:8:2], [sm], [gt2])
                if DNS == 1:
                    return
                tt('dve', dg.h[:], ident_f.unsqueeze(1).broadcast_to([128, 16, 128]),
                   gcb.h[:].unsqueeze(2).broadcast_to([128, 16, 128]), ALU.mult, [cst, gcb], [dg])
                if DNS == 23:
                    return
                psR = [PS.get() for _ in range(4)]
                for q in range(4):
                    mm(psR[q].h[:, :], ones_f, dg.h[:, q * 4:(q + 1) * 4, :], True, True, [cst, dg], [psR[q]])
                if DNS == 24:
                    return
                df = big.get()
                egr = big.get()
                br = big.get()
                for gi in range(2):
                    tt('dve', g4(df, gi), gcb.h[:, gi * 4:gi * 4 + 4].unsqueeze(2).broadcast_to([128, 4, 128]), p4(psR[gi]), ALU.subtract, [gcb, psR[gi]], [df])
                    act(g4(egr, gi), p4(psR[gi]), AF.Exp, [psR[gi]], [egr])
                    cp('act', g4(br, gi), p4(psR[2 + gi]), [psR[2 + gi]], [br])
                for q in range(4):
                    PS.put(psR[q])
                if DNS == 20:
                    return
                e2 = big.get()
                stt(e2.h[:], df.h[:], -1.0, df.h[:], ALU.mult, ALU.max, [df], [e2])
                big.put(df)
                act(e2.h[:], e2.h[:], AF.Exp, [e2], [e2], scale=-1.0)
                e1 = big.get()
                tt('pool', e1.h[:], e2.h[:], ms8, ALU.mult, [e2, mk], [e1])
                dsT = big.get()
                tt('pool', dsT.h[:], e2.h[:], mst8, ALU.mult, [e2, mk], [dsT])
                tt('pool', e2.h[:], e2.h[:], mit8, ALU.mult, [e2, mk], [e2])
                tt('dve', dsT.h[:], dsT.h[:], br.h[:], ALU.mult, [dsT, br], [dsT])
                big.put(br)
                if DNS == 2:
                    return
                Nm = big.get()
                Gm = big.get()
                qkm = bigB.get()
                for d in range(2):
                    for pr in range(2):
                        pb = pr * 64
                        cp('pool', kz.h[pb:pb + 64, d * 4 + pr:d * 4 + pr + 3:2, :], qkfm[par][d].h[pb:pb + 64, 2:4, :], [qkfm[par][d]], [kz])
                psK = [PS.get() for _ in range(2)]
                psQ = [PS.get() for _ in range(2)]
                for u in range(8):
                    d, h = u // 4, u % 4
                    pb = (h % 2) * 64
                    mm(psK[d].h[:, h * 128:(h + 1) * 128], kz.h[:, u, :], qkfm[par][d].h[:, 2 + h // 2, :], True, True, [kz, qkfm[par][d]], [psK[d]])
                    mm(psQ[d].h[:, h * 128:(h + 1) * 128], kz.h[:, u, :], qkfm[par][d].h[:, h // 2, :], True, True, [kz, qkfm[par][d]], [psQ[d]])
                for gi in range(2):
                    tt('dve', g4(Nm, gi), p4(psK[gi]), g4(e1, gi), ALU.mult, [psK[gi], e1], [Nm])
                    tt('dve', g4(Gm, gi), p4(psK[gi]), g4(dsT, gi), ALU.mult, [psK[gi], dsT], [Gm])
                    tt('dve', g4(qkm, gi), p4(psQ[gi]), g4(e2, gi), ALU.mult, [psQ[gi], e2], [qkm])
                    PS.put(psK[gi])
                    PS.put(psQ[gi])
                big.put(e1)
                big.put(e2)
                big.put(dsT)
                if DNS == 22:
                    return
                NmB = bigB.get()
                tt('pool', NmB.h[:], Nm.h[:], gcb.h[:, 8:16].unsqueeze(2).broadcast_to([128, 8, 128]), ALU.mult, [Nm, gcb], [NmB])
                Nb = bigB.get()
                Gb = bigB.get()
                tt('pool', Gb.h[:], Gm.h[:], bc8('bd16'), ALU.mult, [Gm, cst], [Gb])
                tt('pool', Nb.h[:], NmB.h[:], bc8('bd16'), ALU.mult, [NmB, cst], [Nb])
                big.put(Nm)
                big.put(Gm)
                if DNS == 3:
                    return
                Tc = bigB.get()
                tt('dve', Tc.h[:], ident_f.unsqueeze(1).broadcast_to([128, 8, 128]), Gb.h[:], ALU.subtract, [cst, Gb], [Tc])
                G2 = bigB.get()
                N2 = bigB.get()
                stage(Nb, Gb, ev_copy(G2, 'act'))
                stage(Gb, Nb, ev_copy(N2, 'act'))
                bigB.put(Nb)
                bigB.put(Gb)
                Tn = bigB.get()
                stage(N2, Tc, ev_add(Tn, Tc, ALU.add))
                bigB.put(Tc)
                Tc = Tn
                G4 = bigB.get()
                N4 = bigB.get()
                stage(N2, G2, ev_copy(G4, 'act'))
                stage(G2, N2, ev_copy(N4, 'act'))
                bigB.put(G2)
                bigB.put(N2)
                Tn = bigB.get()
                stage(N4, Tc, ev_add(Tn, Tc, ALU.add))
                bigB.put(Tc)
                Tc = Tn
                N8 = bigB.get()
                stage(G4, N4, ev_copy(N8, 'act'))
                bigB.put(G4)
                bigB.put(N4)
                Tn = bigB.get()
                stage(N8, Tc, ev_add(Tn, Tc, ALU.add))
                bigB.put(Tc)
                bigB.put(N8)
                Tc = Tn
                if DNS == 4:
                    return
                for lv in range(3):
                    Tt = bigB.get()
                    Wm = bigB.get()
                    stage(Tc, None, ev_copy(Tt, 'act'), ident=True)
                    stage(NmB, Tc, ev_mask(Wm, ('off32', 'off64', 'off128')[lv]))
                    Tn = bigB.get()
                    stage(Tt, Wm, ev_add(Tn, Tc, ALU.subtract))
                    bigB.put(Tt)
                    bigB.put(Wm)
                    bigB.put(Tc)
                    Tc = Tn
                bigB.put(NmB)
                if DNS == 5:
                    return
                tt('dve', bv.h[:], vtm.h[:], gcb.h[:, 8:16].unsqueeze(2).broadcast_to([128, 8, 64]), ALU.mult, [vtm, gcb], [bv])
                for pr in range(2):
                    pb = pr * 64
                    tt('pool', Bkz.h[:, pr:8:2, pb:pb + 64], ktm.h[:, pr:8:2, :], sm.h[:, 3, pr:8:2].unsqueeze(2).broadcast_to([128, 4, 64]), ALU.mult, [ktm, sm], [Bkz])
                    tt('pool', kdz.h[:, pr:8:2, pb:pb + 64], ktm.h[:, pr:8:2, :], sm.h[:, 1, pr:8:2].unsqueeze(2).broadcast_to([128, 4, 64]), ALU.mult, [ktm, sm], [kdz])
                psU = PS.get()
                for u in range(8):
                    mm(psU.h[:, u * 64:(u + 1) * 64], Tc.h[:, u, :], bv.h[:, u, :], True, True, [Tc, bv], [psU])
                cp('act', usb.h[:], psU.h[:, :].rearrange('p (u c) -> p u c', u=8), [psU], [usb])
                PS.put(psU)
                for gi in range(2):
                    psWk = PS.get()
                    for uu in range(4):
                        u = gi * 4 + uu
                        mm(psWk.h[:, uu * 128:(uu + 1) * 128], Bkz.h[:, u, :], Tc.h[:, u, :], True, True, [Bkz, Tc], [psWk])
                    cp('dve', wkT.h[:, gi * 4:(gi + 1) * 4, :], p4(psWk), [psWk], [wkT])
                    PS.put(psWk)
                bigB.put(Tc)
                for d in range(2):
                    for pr in range(2):
                        pb = pr * 64
                        tt('dve', qdT.h[pb:pb + 64, d * 4 + pr:d * 4 + pr + 3:2, :], qkfm[par][d].h[pb:pb + 64, 0:2, :],
                           egr.h[pb:pb + 64, d * 4 + pr:d * 4 + pr + 3:2, :], ALU.mult, [qkfm[par][d], egr], [qdT])
                big.put(egr)
                if DNS == 6:
                    return
                psW = PS.get()
                for u in range(8):
                    d, h = u // 4, u % 4
                    pb = (h % 2) * 64
                    idx = d * 2 + h // 2
                    mm(psW.h[:, u * 64:(u + 1) * 64], wkT.h[:, u, :], Sb.h[:, idx, :], True, True, [wkT, Sb], [psW])
                tt('dve', wsb.h[:], usb.h[:], psW.h[:, :].rearrange('p (u c) -> p u c', u=8), ALU.subtract, [usb, psW], [wsb])
                PS.put(psW)
                psO = PS.get()
                psS = PS.get()
                for u in range(8):
                    d, h = u // 4, u % 4
                    pb = (h % 2) * 64
                    idx = d * 2 + h // 2
                    mm(psO.h[:, u * 64:(u + 1) * 64], qdT.h[:, u, :], Sb.h[:, idx, :], True, False, [qdT, Sb], [psO])
                    mm(psO.h[:, u * 64:(u + 1) * 64], qkm.h[:, u, :], wsb.h[:, u, :], False, True, [qkm, wsb], [psO])
                    mm(psS.h[:, u * 64:(u + 1) * 64], kdz.h[:, u, :], wsb.h[:, u, :], True, True, [kdz, wsb], [psS])
                bigB.put(qkm)
                for d in range(2):
                    ci = cis[d]
                    if ci in seen:
                        tt('dve', oacc.h[:, ci, :], oacc.h[:, ci, :], psO.h[:, d * 256:(d + 1) * 256], ALU.add, [oacc, psO], [oacc])
                    else:
                        cp('act', oacc.h[:, ci, :], psO.h[:, d * 256:(d + 1) * 256], [psO], [oacc])
                        seen.add(ci)
                PS.put(psO)
                tt('dve', S.h[:], S.h[:], gt2.h[:].unsqueeze(2).broadcast_to([128, 4, 64]), ALU.mult, [S, gt2], [S])
                for pr in range(2):
                    pb = pr * 64
                    tt('dve', S.h[pb:pb + 64, :, :], S.h[pb:pb + 64, :, :], psS.h[pb:pb + 64, :].rearrange('p (u v) -> p u v', u=8)[:, pr:8:2, :], ALU.add, [S, psS], [S])
                PS.put(psS)
                cp('act', Sb.h[:], S.h[:], [S], [Sb])
            if DNS == 7:
                return
            if si > 0:
                st(osd_out[l, si - 1], S.h[:].rearrange('p i v -> p (i v)'), S)
            for ci in range(N):
                tok = s0 + ci * 128
                ld(zt.h[:], tm_raw[tok:tok + 128, 0:256], zt, anchor=odst[0])
                act(osq.h[:], oacc.h[:, ci, :], AF.Square, [oacc], [osq])
                P.op('dve', lambda e, o=oss.h[:], i=osq.h[:].rearrange('p (h d) -> p h d', d=64): e.tensor_reduce(out=o, in_=i, axis=AX.X, op=ALU.add), [osq], [oss])
                act(oss.h[:], oss.h[:], AF.Sqrt, [oss], [oss], bias=smallc.h[:, 0:1], scale=1.0 / 64)
                recip(oss.h[:], oss.h[:], [oss], [oss])
                tt('dve', od.h[:].rearrange('p (h d) -> p h d', d=64), oacc.h[:, ci, :].rearrange('p (h d) -> p h d', d=64),
                   oss.h[:].unsqueeze(2).broadcast_to([128, 4, 64]), ALU.mult, [oacc, oss], [od])
                tt('dve', od.h[:], od.h[:], onw.h[:], ALU.mult, [od, onw], [od])
                act(osq.h[:], zt.h[:], AF.Silu, [zt], [osq])
                tt('dve', od.h[:], od.h[:], osq.h[:], ALU.mult, [od, osq], [od])
                osd_ = odst[ci % 2]
                for c2 in range(2):
                    ps = PS.get()
                    tr(ps.h[:, 0:128], od.h[:, c2 * 128:(c2 + 1) * 128], [od], [ps])
                    cp('act', osd_.h[:, c2, :], ps.h[:, 0:128], [ps], [osd_])
                    PS.put(ps)
                st(obuf_d[:, 0:2, tok:tok + 128], osd_.h[:], osd_, anchor=odst[0])

    import os
    stop = int(os.environ.get('KSTOP', '1000'))
    phases = [lambda: phaseAC(None, 0)]
    for l in range(L):
        phases.append(lambda l=l: ((cast_weights(l + 1) if l + 1 < L else None), sb_off.__setitem__(0, PERM_END), lru(l), P.barrier()))
        phases.append(lambda l=l: (sb_off.__setitem__(0, PERM_END), attention(l), P.barrier()))
        phases.append(lambda l=l: (sb_off.__setitem__(0, PERM_END), deltanet(l), P.barrier()))
        phases.append(lambda l=l: phaseAC(l, l + 1 if l + 1 < L else None))
    for i, ph in enumerate(phases):
        if i >= stop:
            break
        ph()
    P.barrier()
    print('ops', {e: len(P.ops[e]) for e in ENGS}, 'sems', len(P.sems), 'sbuf_max', sb_max[0])
    P.emit()
    return nc


def prep_inputs(inp, cfg):
    L, TS, TP, PAST = cfg['L'], cfg['TS'], cfg['TP'], cfg['PAST']
    f = lambda a: np.ascontiguousarray(np.asarray(a, dtype=np.float32))
    carr, _ = make_consts()
    wsts = np.stack([build_wstream(inp, l) for l in range(L)])
    bmodT = f(inp['b_mod'].reshape(L, 72, 128).transpose(0, 2, 1))
    normwT = f(inp['norm_w'].reshape(L, 3, 8, 128).transpose(0, 3, 1, 2).reshape(L, 128, 24))
    rows = TS // 64
    row = np.repeat(np.arange(rows, dtype=np.float32), 64)
    col = np.tile(np.arange(64, dtype=np.float32), rows)
    freqs = (np.float32(10000.0) ** (-np.arange(16, dtype=np.float32) / np.float32(16))).astype(np.float32)
    ang = np.stack([row[:, None] * freqs, col[:, None] * freqs], axis=1).astype(np.float32)
    rope = f(np.concatenate([np.cos(ang).reshape(TS, 32), np.sin(ang).reshape(TS, 32)], axis=1))
    qkw = f(np.concatenate([np.tile(inp['att_qnorm_w'], (1, 8)), np.tile(inp['att_knorm_w'], (1, 2))], axis=1))
    onw = f(np.tile(inp['dn_onorm_w'], (1, 4)))
    dnvec = f(np.concatenate([inp['dn_a_log'].reshape(L, 8), inp['dn_dt_bias'].reshape(L, 8)], axis=1))
    dncw = f(inp['dn_conv_w'].reshape(L, 4, 6, 128).transpose(0, 3, 2, 1).reshape(L, 128, 24))
    lruw = np.zeros((L, 128, 8, 128), np.float32)
    for gi, nm in enumerate(('lru_wr', 'lru_wi')):
        w = inp[nm]
        for d in range(2):
            for cc in range(2):
                for hb_ in range(2):
                    blk = cc * 2 + hb_
                    lruw[:, hb_ * 64:(hb_ + 1) * 64, gi * 4 + d * 2 + cc, hb_ * 64:(hb_ + 1) * 64] = w[:, d, blk]
    maps = []
    nb = inp['x_sample'].shape[0]
    for b in range(nb):
        xs = inp['x_sample'][b]
        xp = inp['x_prompt'][2 * b:2 * b + 2].reshape(2 * TP, D)
        x = np.concatenate([xs, xp], axis=0)
        xT = f(x.T.reshape(8, 128, -1).transpose(1, 0, 2))
        cond = np.stack([inp['c'][b], inp['c_ctx']], axis=1)
        condT = f(cond.reshape(8, 128, 2).transpose(1, 0, 2))
        lruc = np.zeros((L, 128, 2, 16), np.float32)
        for cc in range(2):
            sl = slice(cc * 128, (cc + 1) * 128)
            lruc[:, :, cc, 0:4] = inp['lru_conv_w'][:, :, sl].transpose(0, 2, 1)
            lruc[:, :, cc, 4] = inp['lru_conv_b'][:, sl]
            lruc[:, :, cc, 5] = inp['lru_br'][:, 0, sl]
            lruc[:, :, cc, 6] = inp['lru_br'][:, 1, sl]
            lruc[:, :, cc, 7] = inp['lru_bi'][:, 0, sl]
            lruc[:, :, cc, 8] = inp['lru_bi'][:, 1, sl]
            lruc[:, :, cc, 9] = inp['lru_lam'][:, 0, sl]
            lruc[:, :, cc, 10] = inp['lru_lam'][:, 1, sl]
            lruc[:, :, cc, 11] = inp['state_lru'][b, :, 0, sl]
            lruc[:, :, cc, 12] = inp['state_lru'][b, :, 1, sl]
        sd = inp['state_delta'][b]
        sd0 = np.zeros((L, 128, 4, 64), np.float32)
        for d in range(2):
            for h in range(4):
                pb = (h % 2) * 64
                sd0[:, pb:pb + 64, d * 2 + h // 2, :] = sd[:, d, h]
        maps.append({
            'xT': xT, 'wst': wsts, 'cst': carr, 'condT': condT, 'bmodT': bmodT, 'normwT': normwT, 'rope': rope,
            'qkw': qkw, 'onw': onw, 'dnvec': dnvec, 'dncw': dncw, 'lruc': lruc, 'lruw': lruw,
            'cachek': f(inp['cache_k'][b].reshape(L, PAST, 128)), 'cachev': f(inp['cache_v'][b].reshape(L, PAST, 128)),
            'sd0': f(sd0.reshape(L, 128, 256)),
        })
    return maps


def gather(res, cfg, nb):
    L, TS, TP = cfg['L'], cfg['TS'], cfg['TP']
    yp = np.zeros((2 * nb, TP, D), np.float32)
    ys = np.zeros((nb, TS, D), np.float32)
    nk = np.zeros((2 * nb, L, TP, 2, 64), np.float32)
    nv = np.zeros((2 * nb, L, TP, 2, 64), np.float32)
    nsd = np.zeros((2 * nb, L, 2, 4, 64, 64), np.float32)
    nsl = np.zeros((2 * nb, L, 2, 256), np.float32)
    for b in range(nb):
        r = res[b]
        y = r['yT'].transpose(1, 0, 2).reshape(D, -1).T
        ys[b] = y[:TS]
        yp[2 * b] = y[TS:TS + TP]
        yp[2 * b + 1] = y[TS + TP:]
        for p in range(2):
            nk[2 * b + p] = r['newk'][:, p * TP:(p + 1) * TP, :].reshape(L, TP, 2, 64)
            nv[2 * b + p] = r['newv'][:, p * TP:(p + 1) * TP, :].reshape(L, TP, 2, 64)
            sdo = r['newsd'][:, p].reshape(L, 128, 4, 64)
            for d in range(2):
                for h in range(4):
                    pb = (h % 2) * 64
                    nsd[2 * b + p, :, d, h] = sdo[:, pb:pb + 64, d * 2 + h // 2, :]
            slo = r['newsl'][:, p]
            for d in range(2):
                for cc in range(2):
                    nsl[2 * b + p, :, d, cc * 128:(cc + 1) * 128] = slo[:, :, d * 2 + cc]
    return yp, ys, nk, nv, nsd, nsl


def kernel(**inputs):
    inp = {k: np.asarray(v) for k, v in inputs.items()}
    nb, TS, _ = inp['x_sample'].shape
    TP = inp['x_prompt'].shape[1]
    L = inp['w_mod'].shape[0]
    PAST = inp['cache_k'].shape[2]
    cfg = dict(L=L, TS=TS, TP=TP, PAST=PAST)
    nc = build(cfg)
    maps = prep_inputs(inp, cfg)
    res = run_bass_kernel_spmd(nc, maps, core_ids=list(range(nb)))
    return gather(res.results, cfg, nb)
```
